# Optimizing a Trainium2 kernel written in Bass

```python
import math
import jax, jax.numpy as jnp
from jax import lax
import numpy as np

D_MODEL = 1024
BATCH = 16
SEQ = 2048
DEPTH = 1

HEAD_DIM = 64
DIFF_WIDTH = D_MODEL // 2
N_DIFF_HEADS = DIFF_WIDTH // (2 * HEAD_DIM)
SB_WIDTH = D_MODEL - DIFF_WIDTH
N_SB_HEADS = SB_WIDTH // HEAD_DIM
MIX_WIDTH = DIFF_WIDTH + SB_WIDTH
D_FF = 4 * D_MODEL
Q_BLOCK = 128
EPS = 1e-6
LAMBDA_STD = 0.1

kernel_name = "hymba_diffattn_stickbreaking_sqrelu"


def rmsnorm(x, gain):
    xf = x.astype(jnp.float32)
    inv = lax.rsqrt(jnp.mean(xf * xf, axis=-1, keepdims=True) + EPS)
    return (xf * inv * gain.astype(jnp.float32)).astype(x.dtype)


def lambda_init_fn(layer_idx):
    return 0.8 - 0.6 * math.exp(-0.3 * layer_idx)


def alibi_slopes(n_heads):
    return jnp.asarray([2.0 ** (-8.0 * (h + 1) / n_heads) for h in range(n_heads)], dtype=jnp.float32)


def diff_attn_block(q1, q2, k1, k2, v, lam, slopes, q0):
    tq, tk = q1.shape[2], k1.shape[2]
    qpos = q0 + jnp.arange(tq)
    kpos = jnp.arange(tk)
    dist = (qpos[:, None] - kpos[None, :]).astype(jnp.float32)
    causal = dist >= 0
    bias = -slopes[:, None, None] * dist
    scale = 1.0 / math.sqrt(HEAD_DIM)

    def probs(q, k):
        s = jnp.einsum('bhqd,bhkd->bhqk', q, k).astype(jnp.float32) * scale + bias
        s = jnp.where(causal, s, -jnp.inf)
        return jax.nn.softmax(s, axis=-1)

    a = probs(q1, k1) - lam * probs(q2, k2)
    return jnp.einsum('bhqk,bhkd->bhqd', a.astype(v.dtype), v)


def stick_breaking_block(q, k, v, q0):
    tq, tk = q.shape[2], k.shape[2]
    qpos = q0 + jnp.arange(tq)
    kpos = jnp.arange(tk)
    strict = kpos[None, :] < qpos[:, None]
    z = jnp.einsum('bhqd,bhkd->bhqk', q, k).astype(jnp.float32) * (1.0 / math.sqrt(HEAD_DIM))
    log_beta = jax.nn.log_sigmoid(z)
    log_om = jnp.where(strict, jax.nn.log_sigmoid(-z), 0.0)
    later = lax.cumsum(log_om, axis=3, reverse=True) - log_om
    w = jnp.where(strict, jnp.exp(log_beta + later), 0.0)
    return jnp.einsum('bhqk,bhkd->bhqd', w.astype(v.dtype), v)


def to_heads(t, n_heads, dim):
    b, s, _ = t.shape
    return t.reshape(b, s, n_heads, dim).transpose(0, 2, 1, 3)


def hybrid_mixer(h, w_in, lq1, lk1, lq2, lk2, diff_g, sb_g, w_out, layer_idx):
    b, s, _ = h.shape
    proj = jnp.einsum('bsd,de->bse', h, w_in)
    dq, dk, dv, sq, sk, sv = jnp.split(
        proj, np.cumsum([DIFF_WIDTH, DIFF_WIDTH, DIFF_WIDTH, SB_WIDTH, SB_WIDTH])[:].tolist(), axis=-1)
    dq = to_heads(dq, 2 * N_DIFF_HEADS, HEAD_DIM)
    dk = to_heads(dk, 2 * N_DIFF_HEADS, HEAD_DIM)
    q1, q2 = dq[:, 0::2], dq[:, 1::2]
    k1, k2 = dk[:, 0::2], dk[:, 1::2]
    dv = to_heads(dv, N_DIFF_HEADS, 2 * HEAD_DIM)
    sq = to_heads(sq, N_SB_HEADS, HEAD_DIM)
    sk = to_heads(sk, N_SB_HEADS, HEAD_DIM)
    sv = to_heads(sv, N_SB_HEADS, HEAD_DIM)

    lam_init = lambda_init_fn(layer_idx)
    lam = (jnp.exp(jnp.sum(lq1.astype(jnp.float32) * lk1.astype(jnp.float32)))
           - jnp.exp(jnp.sum(lq2.astype(jnp.float32) * lk2.astype(jnp.float32)))
           + lam_init)
    slopes = alibi_slopes(N_DIFF_HEADS)

    diff_out, sb_out = [], []
    for i in range(s // Q_BLOCK):
        q0 = i * Q_BLOCK
        qe = q0 + Q_BLOCK
        diff_out.append(diff_attn_block(q1[:, :, q0:qe], q2[:, :, q0:qe], k1[:, :, :qe], k2[:, :, :qe],
                                        dv[:, :, :qe], lam, slopes, q0))
        sb_out.append(stick_breaking_block(sq[:, :, q0:qe], sk[:, :, :qe], sv[:, :, :qe], q0))
    diff_o = jnp.concatenate(diff_out, axis=2)
    sb_o = jnp.concatenate(sb_out, axis=2)

    diff_o = rmsnorm(diff_o, diff_g) * (1.0 - lam_init)
    sb_o = rmsnorm(sb_o, sb_g)
    diff_o = diff_o.transpose(0, 2, 1, 3).reshape(b, s, DIFF_WIDTH)
    sb_o = sb_o.transpose(0, 2, 1, 3).reshape(b, s, SB_WIDTH)
    mixed = jnp.concatenate([diff_o, sb_o], axis=-1)
    return jnp.einsum('bse,ed->bsd', mixed, w_out)


def setup_inputs(seed: int = 0) -> dict:
    key = jax.random.key(seed)
    ks = jax.random.split(key, 14)
    f32 = jnp.float32
    n = jax.random.normal
    return {
        "x": n(ks[0], (BATCH, SEQ, D_MODEL), f32),
        "attn_norm": 1.0 + 0.02 * n(ks[1], (DEPTH, D_MODEL), f32),
        "w_in": n(ks[2], (DEPTH, D_MODEL, 3 * MIX_WIDTH), f32) * D_MODEL ** -0.5,
        "lambda_q1": LAMBDA_STD * n(ks[3], (DEPTH, HEAD_DIM), f32),
        "lambda_k1": LAMBDA_STD * n(ks[4], (DEPTH, HEAD_DIM), f32),
        "lambda_q2": LAMBDA_STD * n(ks[5], (DEPTH, HEAD_DIM), f32),
        "lambda_k2": LAMBDA_STD * n(ks[6], (DEPTH, HEAD_DIM), f32),
        "diff_subln": 1.0 + 0.02 * n(ks[7], (DEPTH, 2 * HEAD_DIM), f32),
        "sb_subln": 1.0 + 0.02 * n(ks[8], (DEPTH, HEAD_DIM), f32),
        "w_out": n(ks[9], (DEPTH, MIX_WIDTH, D_MODEL), f32) * MIX_WIDTH ** -0.5,
        "mlp_norm": 1.0 + 0.02 * n(ks[10], (DEPTH, D_MODEL), f32),
        "w_up": n(ks[11], (DEPTH, D_MODEL, D_FF), f32) * D_MODEL ** -0.5,
        "w_down": n(ks[12], (DEPTH, D_FF, D_MODEL), f32) * D_FF ** -0.5,
        "final_norm": 1.0 + 0.02 * n(ks[13], (D_MODEL,), f32),
    }


def reference(x, attn_norm, w_in, lambda_q1, lambda_k1, lambda_q2, lambda_k2, diff_subln, sb_subln,
              w_out, mlp_norm, w_up, w_down, final_norm):
    h = x
    for l in range(DEPTH):
        a = rmsnorm(h, attn_norm[l])
        h = h + hybrid_mixer(a, w_in[l], lambda_q1[l], lambda_k1[l], lambda_q2[l], lambda_k2[l],
                             diff_subln[l], sb_subln[l], w_out[l], l)
        m = rmsnorm(h, mlp_norm[l])
        u = jnp.square(jax.nn.relu(jnp.einsum('bsd,df->bsf', m, w_up[l])))
        h = h + jnp.einsum('bsf,fd->bsd', u, w_down[l])
    return rmsnorm(h, final_norm)
```

```python
import math
import numpy as np
import ml_dtypes
import concourse.bass as bass
import concourse.mybir as mybir
from concourse.bass_utils import run_bass_kernel_spmd

F32 = mybir.dt.float32
BF16 = mybir.dt.bfloat16
AF = mybir.ActivationFunctionType
ALU = mybir.AluOpType
AX = mybir.AxisListType

D = 1024
S = 2048
NT = S // 128
NCH = S // 512
DFF = 4096
EPS = 1e-6
LAM_INIT = 0.8 - 0.6 * math.exp(0.0)
SLOPES = [2.0 ** (-8.0 * (h + 1) / 4) for h in range(4)]
NEG = -30000.0

C_ID, C_NTRI, C_NONES, C_ONES, C_BONES, C_MD, C_MS = range(7)
CST_W = 7 * 128 + 512 + 512


class Op:
    __slots__ = ("idx", "eng", "fn", "dma", "dma_val", "waits", "signal", "sig")


class Sched:
    ENGS = ("pe", "act", "dve", "pool", "sp")

    def __init__(self):
        self.ops = []
        self.last_w = {}
        self.readers = {}
        self.dma_n = {}
        self.last_eng = {}
        self.last_dma = {}
        self.pending = {e: [] for e in self.ENGS}
        self.extra = {}

    def add(self, eng, fn, reads=(), writes=(), dma=None):
        writes = list(writes) + [k for k in reads if k[0] == "ps"]
        op = Op()
        op.idx = len(self.ops)
        op.eng = eng
        op.fn = fn
        op.dma = dma
        op.dma_val = 0
        op.signal = False
        op.sig = 0
        deps = {}

        def consider(p, kind):
            if p is None:
                return
            if p.dma is None and dma is None and p.eng == eng and kind != "raw":
                return
            key = ("dma", p.dma) if p.dma is not None else p.eng
            cur = deps.get(key)
            if cur is None or cur.idx < p.idx:
                deps[key] = p

        for k in reads:
            consider(self.last_w.get(k), "raw")
        for k in writes:
            consider(self.last_w.get(k), "waw")
            for r in self.readers.get(k, {}).values():
                consider(r, "war")
        for p in self.pending[eng]:
            consider(p, "bar")
        self.pending[eng] = []
        for k in list(reads) + list(writes):
            for p in self.extra.pop(k, ()):
                consider(p, "bar")
        op.waits = list(deps.values())
        for p in op.waits:
            p.signal = True
        if dma is not None:
            self.dma_n[dma] = self.dma_n.get(dma, 0) + 1
            op.dma_val = 16 * self.dma_n[dma]
            self.last_dma[dma] = op
        else:
            self.last_eng[eng] = op
        wset = set(writes)
        for k in wset:
            self.last_w[k] = op
            self.readers[k] = {}
        rk = ("dma", dma) if dma is not None else eng
        for k in reads:
            if k not in wset:
                self.readers.setdefault(k, {})[rk] = op
        self.ops.append(op)
        return op

    def guard(self, new_keys, old_keys):
        ops = []
        for k in old_keys:
            w = self.last_w.get(k)
            if w is not None:
                ops.append(w)
            ops.extend(self.readers.get(k, {}).values())
        for k in new_keys:
            self.extra.setdefault(k, []).extend(ops)

    def barrier(self):
        deps = list(self.last_eng.values()) + list(self.last_dma.values())
        for e in self.ENGS:
            self.pending[e] = list(deps)

    def emit(self, nc, block, sems, dma_sems, final_dma_keys):
        cnt = {e: 0 for e in self.ENGS}
        for op in self.ops:
            if op.dma is None and op.signal:
                cnt[op.eng] += 1
                op.sig = cnt[op.eng]
        per_eng = {e: [o for o in self.ops if o.eng == e] for e in self.ENGS}

        def run(engobj, ename):
            waited = {}
            for op in per_eng[ename]:
                for p in op.waits:
                    if p.dma is not None:
                        sem, val, sk = dma_sems[p.dma], p.dma_val, ("d", p.dma)
                    else:
                        sem, val, sk = sems[p.eng], p.sig, ("e", p.eng)
                    if waited.get(sk, 0) >= val:
                        continue
                    waited[sk] = val
                    engobj.wait_ge(sem, val)
                ins = op.fn(engobj)
                if op.dma is not None:
                    ins.then_inc(dma_sems[op.dma], 16)
                elif op.signal:
                    ins.then_inc(sems[ename], 1)
            if ename == "sp":
                for k in final_dma_keys:
                    if k in self.dma_n:
                        engobj.wait_ge(dma_sems[k], 16 * self.dma_n[k])

        @block.tensor
        def _(e):
            run(e, "pe")

        @block.scalar
        def _(e):
            run(e, "act")

        @block.vector
        def _(e):
            run(e, "dve")

        @block.gpsimd
        def _(e):
            run(e, "pool")

        @block.sync
        def _(e):
            run(e, "sp")


class Rot:
    def __init__(self, items):
        self.items = list(items)
        self.i = 0

    def next(self):
        v = self.items[self.i % len(self.items)]
        self.i += 1
        return v


def make_consts():
    bf = ml_dtypes.bfloat16
    cst = np.zeros((128, CST_W), np.float32)
    i = np.arange(128)[:, None]
    j = np.arange(128)[None, :]
    cst[:, C_ID * 128:(C_ID + 1) * 128] = (i == j)
    cst[:, C_NTRI * 128:(C_NTRI + 1) * 128] = -(i >= j).astype(np.float32)
    cst[:, C_NONES * 128:(C_NONES + 1) * 128] = -1.0
    cst[:, C_ONES * 128:(C_ONES + 1) * 128] = 1.0
    cst[:, C_BONES * 128:(C_BONES + 1) * 128] = ((i // 64) == (j // 64))
    cst[:, C_MD * 128:(C_MD + 1) * 128] = NEG * (i > j)
    cst[:, C_MS * 128:(C_MS + 1) * 128] = NEG * (i >= j)
    o = 7 * 128
    jj = np.arange(512)
    cst[0, o:o + 512] = -(jj - jj % 128)
    cst[1, o:o + 512] = -(jj % 128)
    o2 = o + 512
    for h in range(4):
        cst[0:2, o2 + h * 128:o2 + (h + 1) * 128] = SLOPES[h]
    btab = np.zeros((128, 64), np.float32)
    for h in range(4):
        for d in range(16):
            btab[:, h * 16 + d] = SLOPES[h] * np.arange(128) - SLOPES[h] * 128.0 * (d - 3)
    return cst.astype(bf), btab


class _Stop(Exception):
    pass


def build_nc(nseq=2, stage=99, stage1=99):
    nc = bass.Bass("TRN2", target_bir_lowering=False)
    ntok = nseq * S
    x = nc.dram_tensor("x", [ntok, D], F32, kind="ExternalInput").ap()
    w_in = nc.dram_tensor("w_in", [D, 3 * D], F32, kind="ExternalInput").ap()
    w_out = nc.dram_tensor("w_out", [D, D], F32, kind="ExternalInput").ap()
    w_up = nc.dram_tensor("w_up", [D, DFF], F32, kind="ExternalInput").ap()
    w_down = nc.dram_tensor("w_down", [DFF, D], F32, kind="ExternalInput").ap()
    g_attn_d = nc.dram_tensor("g_attn", [128, D], F32, kind="ExternalInput").ap()
    g_mlp_d = nc.dram_tensor("g_mlp", [128, D], F32, kind="ExternalInput").ap()
    g_fin_d = nc.dram_tensor("g_fin", [128, D], F32, kind="ExternalInput").ap()
    lamv_d = nc.dram_tensor("lamv", [128, 256], F32, kind="ExternalInput").ap()
    gcol_d = nc.dram_tensor("gcol", [128, 2], F32, kind="ExternalInput").ap()
    cst_d = nc.dram_tensor("cst", [128, CST_W], BF16, kind="ExternalInput").ap()
    btab_d = nc.dram_tensor("btab", [128, 64], F32, kind="ExternalInput").ap()
    out = nc.dram_tensor("out", [ntok, D], F32, kind="ExternalOutput").ap()

    wup_sc = nc.dram_tensor("wup_sc", [128, 16 * 8 * 256], BF16).ap()
    wdn_sc = nc.dram_tensor("wdn_sc", [128, 32 * 1024], BF16).ap()
    wout_sc = nc.dram_tensor("wout_sc", [128, 8 * 1024], BF16).ap()
    win_sc = nc.dram_tensor("win_sc", [128, 8 * 3072], BF16).ap()
    w_in_v = w_in.rearrange("(k p) e -> p k e", p=128)
    w_out_v = w_out.rearrange("(k p) e -> p k e", p=128)
    w_up_v = w_up.rearrange("(k p) e -> p k e", p=128)
    w_down_v = w_down.rearrange("(k p) e -> p k e", p=128)

    sc = Sched()

    from contextlib import ExitStack
    with ExitStack() as es_:
        en = es_.enter_context
        R1 = en(nc.sbuf_tensor("R1", [128, 32768], BF16))
        R2 = en(nc.sbuf_tensor("R2", [128, 16384], BF16))
        R3 = en(nc.sbuf_tensor("R3", [128, 24576], BF16))
        R4 = en(nc.sbuf_tensor("R4", [128, 16384], BF16))
        xbuf = en(nc.sbuf_tensor("xbuf", [128, 2, D], F32))
        abf = en(nc.sbuf_tensor("abf", [128, 2, D], BF16))
        gA = en(nc.sbuf_tensor("gA", [128, D], F32))
        gF = en(nc.sbuf_tensor("gF", [128, D], F32))
        cst = en(nc.sbuf_tensor("cst_sb", [128, CST_W], BF16))
        btab = en(nc.sbuf_tensor("btab_sb", [128, 64], F32))
        lamv = en(nc.sbuf_tensor("lamv_sb", [128, 256], F32))
        small = en(nc.sbuf_tensor("small", [128, 64], F32))
        stg = en(nc.sbuf_tensor("stg", [128, 2, 1024], BF16))
        ps = en(nc.psum_tensor("ps", [128, 8, 512], F32))
        s_pe = en(nc.semaphore("s_pe"))
        s_act = en(nc.semaphore("s_act"))
        s_dve = en(nc.semaphore("s_dve"))
        s_pool = en(nc.semaphore("s_pool"))
        s_sp = en(nc.semaphore("s_sp"))
        dma_sems = {}
        for nm in ("x0", "x1", "o0", "o1", "win", "wout", "wdown", "wup0", "wup1", "wup2", "wup3", "cst", "g", "cv0", "cv1", "cs0", "cs1", "wc0", "wc1", "wc2", "wc3", "wc4", "wc5"):
            dma_sems[nm] = en(nc.semaphore("d_" + nm))
        block = en(nc.Block())
        sems = {"pe": s_pe, "act": s_act, "dve": s_dve, "pool": s_pool, "sp": s_sp}

        QT = R1[:, :].rearrange("p (b t) -> p b t", b=16)
        WDN = R1[:, :].rearrange("p (f d) -> p f d", f=32)
        V = R2[:, :].rearrange("p (t e) -> p t e", t=16)
        UT = R2[:, :].rearrange("p (f t) -> p f t", f=32)
        WIN = R3[:, :].rearrange("p (k e) -> p k e", k=8)
        WOUT = R3[:, 0:8192].rearrange("p (k e) -> p k e", k=8)
        WUP = [R3[:, 8192 + i * 4096: 8192 + (i + 1) * 4096].rearrange("p (k e) -> p k e", k=8)
               for i in range(2)]
        WUP4 = [R3[:, 8192 + i * 2048: 8192 + (i + 1) * 2048].rearrange("p (k e) -> p k e", k=8)
                for i in range(4)]
        HB = R3[:, 16384:24576].bitcast(F32).rearrange("p (t d) -> p t d", t=4)
        ACTT = R4[:, :].rearrange("p (k t) -> p k t", k=8)

        KP = [xbuf[:, i, :].bitcast(BF16) for i in range(2)]
        VP = [abf[:, :, :].rearrange("p a b -> p (a b)").rearrange("p (t e) -> p t e", e=128),
              gA[:, :].bitcast(BF16).rearrange("p (t e) -> p t e", e=128)]
        qaugP = cst[:, 7 * 128: 7 * 128 + 512]
        kaugP = [cst[:, 7 * 128 + 512 + h * 128: 7 * 128 + 512 + (h + 1) * 128] for h in range(4)]
        off = [0]

        def carve(nel, dt=BF16, shape=None):
            n16 = nel if dt == BF16 else nel * 2
            a = R3[:, off[0]:off[0] + n16]
            off[0] += n16
            if dt == F32:
                a = a.bitcast(F32)
            if shape is not None:
                a = a.rearrange("p (a b) -> p a b", a=shape[0])
            return a

        Pb = [carve(1024, BF16, (2, 512)) for _ in range(2)]
        LN1 = [carve(512, F32) for _ in range(2)]
        LN2 = [carve(512, F32) for _ in range(2)]
        O1S = [carve(512, F32) for _ in range(2)]
        O2S = [carve(512, F32) for _ in range(2)]
        OSQ = [carve(512, BF16) for _ in range(2)]
        RINV = [carve(512, F32) for _ in range(2)]
        Eb = [carve(1024, F32, (2, 512)) for _ in range(2)]
        Lb = [carve(1024, BF16, (2, 512)) for _ in range(3)]
        Wb = [carve(1024, BF16, (2, 512)) for _ in range(2)]
        Rs = [carve(1024, BF16, (2, 512)) for _ in range(2)]
        assert off[0] <= 24576

        def cblk(i):
            return cst[:, i * 128:(i + 1) * 128]

        ident = cblk(C_ID)
        ntri = cblk(C_NTRI)
        nones = cblk(C_NONES)
        ones = cblk(C_ONES)
        bones = cblk(C_BONES)
        maskd = cblk(C_MD)
        masks = cblk(C_MS)
        qaug = cst[0:2, 7 * 128: 7 * 128 + 512]
        kaug = [cst[0:2, 7 * 128 + 512 + h * 128: 7 * 128 + 512 + (h + 1) * 128] for h in range(4)]

        SS = small[:, 0:8]
        RSTD = small[:, 8:16]
        LAM = small[:, 16:24]
        GCOL = small[:, 24:26]
        LTMP = small[:, 32:64]
        lamtmp = lamv

        sc.add("sp", lambda e: e.dma_start(out=cst[:, :], in_=cst_d), writes=[("cst",)], dma="cst")
        sc.add("sp", lambda e: e.dma_start(out=btab[:, :], in_=btab_d), writes=[("cst2",)], dma="cst")
        sc.add("sp", lambda e: e.dma_start(out=lamv[:, :], in_=lamv_d), writes=[("lamv",)], dma="cst")
        sc.add("sp", lambda e: e.dma_start(out=GCOL, in_=gcol_d), writes=[("gcol",)], dma="cst")
        sc.add("sp", lambda e: e.dma_start(out=gF[:, :], in_=g_fin_d), writes=[("gF",)], dma="cst")
        for op_ in sc.ops:
            if op_.dma == "cst":
                op_.dma_val = 16 * sc.dma_n["cst"]
        sc.add("dve", lambda e: e.tensor_tensor(out=lamv[:, 0:64], in0=lamv[:, 0:64], in1=lamv[:, 64:128], op=ALU.mult),
               reads=[("lamv",)], writes=[("lamv",)])
        sc.add("dve", lambda e: e.tensor_tensor(out=lamv[:, 128:192], in0=lamv[:, 128:192], in1=lamv[:, 192:256], op=ALU.mult),
               reads=[("lamv",)], writes=[("lamv",)])
        sc.add("dve", lambda e: e.reduce_sum(out=LAM[:, 0:1], in_=lamv[:, 0:64], axis=AX.X),
               reads=[("lamv",)], writes=[("lam0",)])
        sc.add("dve", lambda e: e.reduce_sum(out=LAM[:, 1:2], in_=lamv[:, 128:192], axis=AX.X),
               reads=[("lamv",)], writes=[("lam1",)])
        sc.add("act", lambda e: e.activation(out=LAM[:, 2:4], in_=LAM[:, 0:2], func=AF.Exp),
               reads=[("lam0",), ("lam1",)], writes=[("lam2",)])
        sc.add("dve", lambda e: e.tensor_tensor(out=LAM[:, 4:5], in0=LAM[:, 3:4], in1=LAM[:, 2:3], op=ALU.subtract),
               reads=[("lam2",)], writes=[("lam4",)])
        sc.add("dve", lambda e: e.tensor_scalar(out=LAM[:, 5:6], in0=LAM[:, 4:5], scalar1=-LAM_INIT, scalar2=None, op0=ALU.add),
               reads=[("lam4",)], writes=[("neglam",)])
        sc.add("dve", lambda e: e.tensor_scalar(out=GCOL[:, 0:1], in0=GCOL[:, 0:1], scalar1=(1.0 - LAM_INIT) * math.sqrt(128.0), scalar2=None, op0=ALU.mult),
               reads=[("gcol",)], writes=[("gcol",)])
        sc.add("dve", lambda e: e.tensor_scalar(out=GCOL[:, 1:2], in0=GCOL[:, 1:2], scalar1=8.0, scalar2=None, op0=ALU.mult),
               reads=[("gcol",)], writes=[("gcol",)])
        NEGLAM = LAM[:, 5:6]

        cnt = {"ss": 0, "x": 0, "ev": 0}

        def rms_to_bf16(xs_ap, xkey, gain_ap, gkey, dst_ap, dkey, junk_ap, junkkey):
            i = cnt["ss"] % 8
            cnt["ss"] += 1
            sc.add("act", lambda e: e.activation(out=junk_ap, in_=xs_ap, func=AF.Square, accum_out=SS[:, i:i + 1]),
                   reads=list(xkey), writes=[junkkey, ("ss", i)])
            sc.add("dve", lambda e: e.tensor_scalar(out=RSTD[:, i:i + 1], in0=SS[:, i:i + 1], scalar1=1.0 / D, scalar2=EPS,
                                                    op0=ALU.mult, op1=ALU.add),
                   reads=[("ss", i)], writes=[("rstd", i)])
            sc.add("act", lambda e: e.activation(out=RSTD[:, i:i + 1], in_=RSTD[:, i:i + 1], func=AF.Ln),
                   reads=[("rstd", i)], writes=[("rstd", i)])
            sc.add("act", lambda e: e.activation(out=RSTD[:, i:i + 1], in_=RSTD[:, i:i + 1], func=AF.Exp, scale=-0.5),
                   reads=[("rstd", i)], writes=[("rstd", i)])
            sc.add("dve", lambda e: e.scalar_tensor_tensor(out=dst_ap, in0=xs_ap, scalar=RSTD[:, i:i + 1], in1=gain_ap,
                                                           op0=ALU.mult, op1=ALU.mult),
                   reads=list(xkey) + [("rstd", i), gkey], writes=[dkey])

        tp_banks = Rot([0, 1])

        def transpose_to_actT(src_bf, srckey, T, rot=None):
            b = (rot or tp_banks).next()
            tpv = ps[:, b, :].bitcast(BF16).rearrange("p (k t) -> p k t", k=8)

            def f(e):
                ins = None
                for k in range(8):
                    ins = e.transpose(out=tpv[:, k, :], in_=src_bf[:, k * 128:(k + 1) * 128], identity=ident)
                return ins
            sc.add("pe", f, reads=[srckey, ("cst",)], writes=[("ps", b)])
            dst = ACTT[:, :, T * 128:(T + 1) * 128]
            if cnt["ev"] % 2 == 0:
                sc.add("act", lambda e: e.activation(out=dst, in_=tpv, func=AF.Copy), reads=[("ps", b)], writes=[("act", T)])
            else:
                sc.add("dve", lambda e: e.tensor_copy(out=dst, in_=tpv), reads=[("ps", b)], writes=[("act", T)])
            cnt["ev"] += 1

        def evac(dst, dkeys, bank, scale=1.0):
            src = ps[:, bank, :]
            if cnt["ev"] % 2 == 0:
                sc.add("act", lambda e: e.activation(out=dst, in_=src, func=AF.Copy, scale=scale),
                       reads=[("ps", bank)], writes=dkeys)
            else:
                sc.add("dve", lambda e: e.tensor_scalar(out=dst, in0=src, scalar1=scale, scalar2=None, op0=ALU.mult),
                       reads=[("ps", bank)], writes=dkeys)
            cnt["ev"] += 1

        def load_x(row0, slot):
            sc.add("sp", lambda e: e.dma_start(out=xbuf[:, slot, :], in_=x[row0:row0 + 128, :]),
                   writes=[("x", slot)], dma="x%d" % slot)

        conv_jobs = [("out", k, 0) for k in range(8)] + [("up", k, j) for j in range(8) for k in range(8)] \
            + [("dn", fb, 0) for fb in range(32)] + ([("in", k, j) for k in range(8) for j in range(3)] if nseq > 1 else [])
        conv_n = [0]
        WSC_KEYS = [("wsc", nm, sl) for nm in ("up", "dn", "out", "in") for sl in range(2)]

        def emit_conv():
            if not conv_jobs:
                return
            kind, a, b = conv_jobs.pop(0)
            sl = conv_n[0] % 2
            conv_n[0] += 1
            if kind == "up":
                k, j = a, b
                sc.add("pool", lambda e: e.dma_start(out=stg[:, sl, 0:512], in_=w_up_v[:, k, j * 512:(j + 1) * 512]),
                       writes=[("stg", sl)], dma="cv%d" % sl)
                for hh in range(2):
                    o = ((2 * j + hh) * 8 + k) * 256
                    sc.add("sp", lambda e, o=o, hh=hh: e.dma_start(out=wup_sc[:, o:o + 256], in_=stg[:, sl, hh * 256:(hh + 1) * 256]),
                           reads=[("stg", sl)], writes=[("wsc", "up", sl)], dma="cs%d" % sl)
            elif kind == "in":
                k, j = a, b
                sc.add("pool", lambda e: e.dma_start(out=stg[:, sl, :], in_=w_in_v[:, k, j * 1024:(j + 1) * 1024]),
                       writes=[("stg", sl)], dma="cv%d" % sl)
                o = k * 3072 + j * 1024
                sc.add("sp", lambda e: e.dma_start(out=win_sc[:, o:o + 1024], in_=stg[:, sl, :]),
                       reads=[("stg", sl)], writes=[("wsc", "in", sl)], dma="cs%d" % sl)
            elif kind == "dn":
                fb = a
                sc.add("pool", lambda e: e.dma_start(out=stg[:, sl, :], in_=w_down_v[:, fb, :]),
                       writes=[("stg", sl)], dma="cv%d" % sl)
                sc.add("sp", lambda e: e.dma_start(out=wdn_sc[:, fb * 1024:(fb + 1) * 1024], in_=stg[:, sl, :]),
                       reads=[("stg", sl)], writes=[("wsc", "dn", sl)], dma="cs%d" % sl)
            else:
                k = a
                sc.add("pool", lambda e: e.dma_start(out=stg[:, sl, :], in_=w_out_v[:, k, :]),
                       writes=[("stg", sl)], dma="cv%d" % sl)
                sc.add("sp", lambda e: e.dma_start(out=wout_sc[:, k * 1024:(k + 1) * 1024], in_=stg[:, sl, :]),
                       reads=[("stg", sl)], writes=[("wsc", "out", sl)], dma="cs%d" % sl)

        QT_KEYS = [("QT", b_, c_) for b_ in range(16) for c_ in range(NCH)]
        V_KEYS = [("V", t_, h_) for t_ in range(NT) for h_ in range(2)]
        WORK_KEYS = ([("P", i_) for i_ in range(2)] + [(n_, i_) for n_ in ("ln1", "ln2", "o1s", "o2s", "osq", "rinv", "E", "W", "R")
                                                      for i_ in range(2)] + [("L", i_) for i_ in range(3)])
        CBUF_KEYS = [("wout",)] + [("wup", i_) for i_ in range(4)] + [("h", t_, h_) for t_ in range(4) for h_ in range(2)]
        UT_KEYS = [("uT", f_) for f_ in range(32)]

        for s in range(nseq):
          try:
            tok0 = s * S
            if s >= 1:
                stage = stage1
            if stage < 1:
                raise _Stop()
            sc.guard(QT_KEYS, [("wdown",)])
            sc.guard(V_KEYS, UT_KEYS)
            sc.guard([("win", 0)], [("wout",)])
            sc.guard([("win", 1)], [("wup", i_) for i_ in range(4)])
            sc.guard([("win", 2)], [("h", t_, h_) for t_ in range(4) for h_ in range(2)])
            sc.add("sp", lambda e: e.dma_start(out=gA[:, :], in_=g_attn_d), writes=[("gA",)], dma="g")
            if s == 0:
                for k in range(8):
                    sc.add("pool", lambda e, k=k: e.dma_start(out=WIN[:, k, :], in_=w_in_v[:, k, :]),
                           writes=[("win", 0), ("win", 1), ("win", 2)], dma="win")
                WINK = lambda col: [("win", 0), ("win", 1), ("win", 2)]
            else:
                for i3 in range(3):
                    sc.add("pool", lambda e, i3=i3: e.dma_start(out=R3[:, i3 * 8192:(i3 + 1) * 8192],
                                                                in_=win_sc[:, i3 * 8192:(i3 + 1) * 8192]),
                           reads=WSC_KEYS, writes=[("win", i3)], dma="win")
                WINK = lambda col: [("win", 0), ("win", 1), ("win", 2)]
            pj_banks = Rot([2, 3, 4, 5, 6, 7])

            def a_chain(c, t):
                slot = cnt["x"] % 2
                cnt["x"] += 1
                load_x(tok0 + (4 * c + t) * 128, slot)
                rms_to_bf16(xbuf[:, slot, :], [("x", slot)], gA[:, :], ("gA",), abf[:, slot, :], ("abf", slot),
                            abf[:, slot, :], ("abf", slot))
                return (abf[:, slot, :], ("abf", slot), 4 * (c % 2) + t)

            def a_groups(c):
                j = c % 2
                akeys = [("act", 4 * j + t) for t in range(4)]
                AT = ACTT[:, :, j * 512:(j + 1) * 512]
                groups = []
                fm = []
                for b in range(4):
                    fm.append((128 * b, b, 0.125))
                for b in range(4):
                    fm.append((512 + 128 * b, 4 + b, 1.0))
                for b in range(4):
                    fm.append((1536 + 128 * b, 8 + b, 0.125))
                for b in range(4):
                    fm.append((2048 + 128 * b, 12 + b, 1.0))
                for (col, blk, scl) in fm:
                    def g(col=col, blk=blk, scl=scl):
                        bank = pj_banks.next()

                        def f(e):
                            ins = None
                            for k in range(8):
                                ins = e.matmul(ps[:, bank, :], lhsT=WIN[:, k, col:col + 128], rhs=AT[:, k, :],
                                               start=(k == 0), stop=(k == 7))
                            return ins
                        sc.add("pe", f, reads=WINK(col) + akeys, writes=[("ps", bank)])
                        evac(QT[:, blk, c * 512:(c + 1) * 512], [("QT", blk, c)], bank, scl)
                    groups.append(g)
                for t in range(4):
                    for half, col in ((0, 1024), (1, 2560)):
                        def g(t=t, half=half, col=col):
                            T = 4 * c + t
                            bank = pj_banks.next()

                            def f(e):
                                ins = None
                                for k in range(8):
                                    ins = e.matmul(ps[:, bank, :], lhsT=AT[:, k, t * 128:(t + 1) * 128],
                                                   rhs=WIN[:, k, col:col + 512], start=(k == 0), stop=(k == 7))
                                return ins
                            sc.add("pe", f, reads=WINK(col) + akeys, writes=[("ps", bank)])
                            evac(V[:, T, half * 512:(half + 1) * 512], [("V", T, half)], bank)
                        groups.append(g)
                return groups

            for t in range(4):
                transpose_to_actT(*a_chain(0, t))
            for c in range(NCH):
                groups = a_groups(c)
                pend_a = None
                for gi_, g in enumerate(groups):
                    if c + 1 < NCH and gi_ % 6 == 0:
                        new_a = a_chain(c + 1, gi_ // 6)
                        if pend_a is not None:
                            transpose_to_actT(*pend_a)
                        pend_a = new_a
                    g()
                if pend_a is not None:
                    transpose_to_actT(*pend_a)

            if stage < 2:
                raise _Stop()
            sc.guard(WORK_KEYS, [("win", 0), ("win", 1), ("win", 2)] + [("winc", cb_) for cb_ in range(6)])
            sc.add("dve", lambda e: e.memset(KP[0][64:128, :], 0.0), writes=[("x", 0)])
            sc.add("dve", lambda e: e.memset(KP[1][0:64, :], 0.0), writes=[("x", 1)])
            sc.add("dve", lambda e: e.memset(VP[0][:, :, 64:128], 0.0), writes=[("abf", 0), ("abf", 1)])
            sc.add("dve", lambda e: e.memset(VP[1][:, :, 0:64], 0.0), writes=[("gA",)])

            def load_kpad(blk):
                sc.add("dve", lambda e: e.tensor_copy(out=KP[0][0:64, :], in_=QT[0:64, blk, :]),
                       reads=[("QT", blk, cc) for cc in range(NCH)], writes=[("x", 0)])
                sc.add("dve", lambda e: e.tensor_copy(out=KP[1][64:128, :], in_=QT[64:128, blk, :]),
                       reads=[("QT", blk, cc) for cc in range(NCH)], writes=[("x", 1)])
            deferred = []

            def defer(it, fn):
                deferred.append((it, fn))

            def run_deferred(it):
                keep = []
                for (d, fn) in deferred:
                    if d <= it:
                        fn()
                    else:
                        keep.append((d, fn))
                deferred[:] = keep

            dunits = [(h, c, kb) for h in range(4) for c in range(NCH) for kb in range(4 * c + 4)]
            nd = len(dunits)
            grp = [0]

            def d_S(i):
                h, c, kb = dunits[i]
                r = max(0, kb - 4 * c)
                col0 = 128 * r
                q = i % 2
                diag = kb >= 4 * c
                if i == 0 or dunits[i - 1][0] != h:
                    load_kpad(4 + h)
                def f(e):
                    ins = None
                    for sm in range(2):
                        bank = 2 * q + sm
                        ins = e.matmul(ps[:, bank, col0:512], lhsT=KP[sm][:, kb * 128:(kb + 1) * 128],
                                       rhs=QT[:, h, c * 512 + col0:(c + 1) * 512], start=True,
                                       stop=(h > 0 and not diag))
                    if h == 0:
                        for sm in range(2):
                            ins = e.matmul(ps[:, 2 * q + sm, col0:512], lhsT=kaugP[h], rhs=qaugP[:, col0:512],
                                           start=False, stop=(not diag))
                    if diag:
                        for sm in range(2):
                            ins = e.matmul(ps[:, 2 * q + sm, col0:col0 + 128], lhsT=ident, rhs=maskd,
                                           start=False, stop=True)
                    return ins
                sc.add("pe", f, reads=[("x", 0), ("x", 1), ("QT", h, c), ("cst",)],
                       writes=[("ps", 2 * q), ("ps", 2 * q + 1)])

            def d_exp(i):
                h, c, kb = dunits[i]
                col0 = 128 * max(0, kb - 4 * c)
                q = i % 2
                sl = i % 2
                bidx = h * 16 + (4 * c - kb + 3)
                sc.add("act", lambda e: e.activation(
                    out=Pb[sl][:, :, col0:512], in_=ps[:, 2 * q:2 * q + 2, col0:512], func=AF.Exp,
                    bias=btab[:, bidx:bidx + 1]),
                    reads=[("ps", 2 * q), ("ps", 2 * q + 1), ("cst2",)], writes=[("P", sl)])

            def d_AV(i, it):
                h, c, kb = dunits[i]
                col0 = 128 * max(0, kb - 4 * c)
                nkb = 4 * c + 4
                sl = i % 2

                def f(e):
                    st = (kb == 0)
                    sp_ = (kb == nkb - 1)
                    e.matmul(ps[:, 4, col0:512], lhsT=V[:, kb, h * 128:(h + 1) * 128], rhs=Pb[sl][:, 0, col0:512],
                             start=st, stop=sp_)
                    e.matmul(ps[:, 6, col0:512], lhsT=ones, rhs=Pb[sl][:, 0, col0:512], start=st, stop=sp_)
                    e.matmul(ps[:, 5, col0:512], lhsT=V[:, kb, h * 128:(h + 1) * 128], rhs=Pb[sl][:, 1, col0:512],
                             start=st, stop=sp_)
                    return e.matmul(ps[:, 7, col0:512], lhsT=ones, rhs=Pb[sl][:, 1, col0:512], start=st, stop=sp_)
                sc.add("pe", f, reads=[("P", sl), ("V", kb, 0), ("cst",)],
                       writes=[("ps", 4), ("ps", 5), ("ps", 6), ("ps", 7)])
                if kb == nkb - 1:
                    d_norm(h, c, it)

            def d_norm(h, c, it):
                m = grp[0] % 2
                grp[0] += 1
                sc.add("act", lambda e: e.activation(out=LN1[m], in_=ps[:, 6, :], func=AF.Ln), reads=[("ps", 6)], writes=[("ln1", m)])
                sc.add("act", lambda e: e.activation(out=LN2[m], in_=ps[:, 7, :], func=AF.Ln), reads=[("ps", 7)], writes=[("ln2", m)])
                sc.add("dve", lambda e: e.tensor_copy(out=O1S[m], in_=ps[:, 4, :]), reads=[("ps", 4)], writes=[("o1s", m)])
                sc.add("dve", lambda e: e.tensor_copy(out=O2S[m], in_=ps[:, 5, :]), reads=[("ps", 5)], writes=[("o2s", m)])

                def n2():
                    sc.add("act", lambda e: e.activation(out=LN1[m], in_=LN1[m], func=AF.Exp, scale=-1.0),
                           reads=[("ln1", m)], writes=[("ln1", m)])
                    sc.add("act", lambda e: e.activation(out=LN2[m], in_=LN2[m], func=AF.Exp, scale=-1.0),
                           reads=[("ln2", m)], writes=[("ln2", m)])

                def n3():
                    sc.add("dve", lambda e: e.tensor_tensor(out=O1S[m], in0=O1S[m], in1=LN1[m], op=ALU.mult),
                           reads=[("o1s", m), ("ln1", m)], writes=[("o1s", m)])
                    sc.add("dve", lambda e: e.tensor_tensor(out=O2S[m], in0=O2S[m], in1=LN2[m], op=ALU.mult),
                           reads=[("o2s", m), ("ln2", m)], writes=[("o2s", m)])
                    sc.add("dve", lambda e: e.scalar_tensor_tensor(out=O1S[m], in0=O2S[m], scalar=NEGLAM, in1=O1S[m],
                                                                   op0=ALU.mult, op1=ALU.add),
                           reads=[("o1s", m), ("o2s", m), ("neglam",)], writes=[("o1s", m)])

                def n4():
                    sc.add("dve", lambda e: e.tensor_tensor(out=OSQ[m], in0=O1S[m], in1=O1S[m], op=ALU.mult),
                           reads=[("o1s", m)], writes=[("osq", m)])

                def n5(it5):
                    q = (it5 + 1) % 2
                    sc.add("pe", lambda e: e.matmul(ps[:, 2 * q, :], lhsT=ones, rhs=OSQ[m], start=True, stop=True),
                           reads=[("osq", m), ("cst",)], writes=[("ps", 2 * q), ("ps", 2 * q + 1)])
                    sc.add("dve", lambda e: e.tensor_scalar(out=RINV[m], in0=ps[:, 2 * q, :], scalar1=128.0 * EPS,
                                                            scalar2=None, op0=ALU.add),
                           reads=[("ps", 2 * q)], writes=[("rinv", m)])

                def n6():
                    sc.add("act", lambda e: e.activation(out=RINV[m], in_=RINV[m], func=AF.Ln),
                           reads=[("rinv", m)], writes=[("rinv", m)])
                    sc.add("act", lambda e: e.activation(out=RINV[m], in_=RINV[m], func=AF.Exp, scale=-0.5),
                           reads=[("rinv", m)], writes=[("rinv", m)])

                def n7():
                    sc.add("dve", lambda e: e.scalar_tensor_tensor(
                        out=ACTT[:, h, c * 512:(c + 1) * 512], in0=O1S[m], scalar=GCOL[:, 0:1], in1=RINV[m],
                        op0=ALU.mult, op1=ALU.mult),
                        reads=[("o1s", m), ("rinv", m), ("gcol",)], writes=[("act", 4 * c + t) for t in range(4)])
                defer(it + 1, n2)
                defer(it + 2, n3)
                defer(it + 3, n4)
                defer(it + 4, lambda: n5(it + 4))
                defer(it + 5, n6)
                defer(it + 6, n7)

            for it in range(nd + 8):
                emit_conv()
                if it < nd:
                    d_S(it)
                    d_exp(it)
                if 1 <= it <= nd:
                    d_AV(it - 1, it)
                run_deferred(it)
            assert not deferred

            if stage < 3:
                raise _Stop()
            sunits = []
            for p in range(4):
                for c in range(NCH):
                    nkb = 4 * c + 4
                    for ui, kb in enumerate(range(nkb - 1, -1, -1)):
                        sunits.append((c, p, kb, ui, len(sunits) - ui))
            ns = len(sunits)
            sgrp = {}
            gi = 0
            for i, (c, p, kb, ui, first) in enumerate(sunits):
                if ui == 0:
                    sgrp[first] = gi
                    gi += 1

            def s_ginfo(i):
                c, p, kb, ui, first = sunits[i]
                g = sgrp[first]
                return g % 2, 6 + (g % 2)

            def s_z(i):
                c, p, kb, ui, first = sunits[i]
                col0 = 128 * max(0, kb - 4 * c)
                diag = kb >= 4 * c
                q = i % 3
                rs, ab = s_ginfo(i)
                if i == 0 or sunits[i - 1][1] != p:
                    load_kpad(12 + p)
                if ui == 0:
                    sc.add("dve", lambda e: e.memset(Rs[rs], 0.0), writes=[("R", rs)])
                def f(e):
                    ins = None
                    for e_ in range(2):
                        ins = e.matmul(ps[:, 2 * q + e_, col0:512], lhsT=KP[e_][:, kb * 128:(kb + 1) * 128],
                                       rhs=QT[:, 8 + p, c * 512 + col0:(c + 1) * 512], start=True, stop=False,
                                       skip_group_check=True)
                    if diag:
                        for e_ in range(2):
                            ins = e.matmul(ps[:, 2 * q + e_, col0:col0 + 128], lhsT=ident, rhs=masks,
                                           start=False, stop=False, skip_group_check=True)
                    return ins
                sc.add("pe", f, reads=[("x", 0), ("x", 1), ("QT", 8 + p, c), ("cst",)],
                       writes=[("ps", 2 * q), ("ps", 2 * q + 1)])

            def s_EL(i):
                c, p, kb, ui, first = sunits[i]
                col0 = 128 * max(0, kb - 4 * c)
                q = i % 3
                es = i % 2
                ls = i % 3
                zkeys = [("ps", 2 * q), ("ps", 2 * q + 1)]
                sc.add("act", lambda e: e.activation(
                    out=Eb[es][:, :, col0:512], in_=ps[:, 2 * q:2 * q + 2, col0:512], func=AF.Exp),
                    reads=zkeys, writes=[("E", es)])
                sc.add("act", lambda e: e.activation(
                    out=Lb[ls][:, :, col0:512], in_=Eb[es][:, :, col0:512], func=AF.Ln, bias=1.0),
                    reads=[("E", es)], writes=[("L", ls)])

            def s_tri(i):
                c, p, kb, ui, first = sunits[i]
                col0 = 128 * max(0, kb - 4 * c)
                q = i % 3
                ls = i % 3
                ws = i % 2
                rs, ab = s_ginfo(i)
                def f(e):
                    ins = None
                    for e_ in range(2):
                        ins = e.matmul(ps[:, 2 * q + e_, col0:512], lhsT=ntri, rhs=Lb[ls][:, e_, col0:512],
                                       start=False, stop=(ui == 0), skip_group_check=True)
                    if ui > 0:
                        for e_ in range(2):
                            ins = e.matmul(ps[:, 2 * q + e_, col0:512], lhsT=nones, rhs=Rs[rs][:, e_, col0:512],
                                           start=False, stop=True, skip_group_check=True)
                    return ins
                sc.add("pe", f, reads=[("L", ls), ("R", rs), ("cst",)], writes=[("ps", 2 * q), ("ps", 2 * q + 1)])
                zkeys = [("ps", 2 * q), ("ps", 2 * q + 1)]
                sc.add("act", lambda e: e.activation(
                    out=Wb[ws][:, :, col0:512], in_=ps[:, 2 * q:2 * q + 2, col0:512], func=AF.Exp),
                    reads=zkeys, writes=[("W", ws)])
                if kb > 0:
                    sc.add("dve", lambda e: e.tensor_tensor(
                        out=Rs[rs][:, :, col0:512], in0=Rs[rs][:, :, col0:512], in1=Lb[ls][:, :, col0:512], op=ALU.add),
                        reads=[("R", rs), ("L", ls)], writes=[("R", rs)])

            def s_AV(i, it):
                c, p, kb, ui, first = sunits[i]
                col0 = 128 * max(0, kb - 4 * c)
                ws = i % 2
                rs, ab = s_ginfo(i)
                if i == 0 or sunits[i - 1][1] != p:
                    vkeys = [("V", T_, 1) for T_ in range(NT)]
                    sc.add("dve", lambda e: e.tensor_copy(out=VP[0][:, :, 0:64], in_=V[:, :, 512 + 128 * p:512 + 128 * p + 64]),
                           reads=vkeys, writes=[("abf", 0), ("abf", 1)])
                    sc.add("dve", lambda e: e.tensor_copy(out=VP[1][:, :, 64:128], in_=V[:, :, 512 + 128 * p + 64:512 + 128 * p + 128]),
                           reads=vkeys, writes=[("gA",)])

                def f(e):
                    ins = None
                    for e_ in range(2):
                        ins = e.matmul(ps[:, ab, col0:512], lhsT=VP[e_][:, kb, :],
                                       rhs=Wb[ws][:, e_, col0:512], start=(ui == 0 and e_ == 0),
                                       stop=(kb == 0 and e_ == 1), skip_group_check=True)
                    return ins
                sc.add("pe", f, reads=[("W", ws), ("abf", 0), ("abf", 1), ("gA",)], writes=[("ps", ab)])
                if kb == 0:
                    s_norm(c, p, ab, it)

            def s_norm(c, p, ab, it):
                m = grp[0] % 2
                grp[0] += 1
                sc.add("dve", lambda e: e.tensor_copy(out=O1S[m], in_=ps[:, ab, :]), reads=[("ps", ab)], writes=[("o1s", m)])
                sc.add("dve", lambda e: e.tensor_tensor(out=OSQ[m], in0=O1S[m], in1=O1S[m], op=ALU.mult),
                       reads=[("o1s", m)], writes=[("osq", m)])

                def m2(it2):
                    q = (it2 + 1) % 3
                    sc.add("pe", lambda e: e.matmul(ps[:, 2 * q, :], lhsT=bones, rhs=OSQ[m], start=True, stop=True),
                           reads=[("osq", m), ("cst",)], writes=[("ps", 2 * q), ("ps", 2 * q + 1)])
                    sc.add("dve", lambda e: e.tensor_scalar(out=RINV[m], in0=ps[:, 2 * q, :], scalar1=64.0 * EPS,
                                                            scalar2=None, op0=ALU.add),
                           reads=[("ps", 2 * q)], writes=[("rinv", m)])

                def m3():
                    sc.add("act", lambda e: e.activation(out=RINV[m], in_=RINV[m], func=AF.Ln),
                           reads=[("rinv", m)], writes=[("rinv", m)])
                    sc.add("act", lambda e: e.activation(out=RINV[m], in_=RINV[m], func=AF.Exp, scale=-0.5),
                           reads=[("rinv", m)], writes=[("rinv", m)])

                def m4():
                    sc.add("dve", lambda e: e.scalar_tensor_tensor(
                        out=ACTT[:, 4 + p, c * 512:(c + 1) * 512], in0=O1S[m], scalar=GCOL[:, 1:2], in1=RINV[m],
                        op0=ALU.mult, op1=ALU.mult),
                        reads=[("o1s", m), ("rinv", m), ("gcol",)], writes=[("act", 4 * c + t) for t in range(4)])
                defer(it + 1, lambda: m2(it + 1))
                defer(it + 2, m3)
                defer(it + 3, m4)

            for it in range(ns + 6):
                emit_conv()
                if it < ns:
                    s_z(it)
                    s_EL(it)
                if 1 <= it <= ns:
                    s_tri(it - 1)
                if 2 <= it <= ns + 1:
                    s_AV(it - 2, it)
                run_deferred(it)
            assert not deferred

            if stage < 4:
                raise _Stop()
            sc.guard([("wdown",)], QT_KEYS)
            sc.guard(UT_KEYS, V_KEYS)
            sc.guard(CBUF_KEYS, WORK_KEYS)
            sc.add("sp", lambda e: e.dma_start(out=gA[:, :], in_=g_mlp_d), writes=[("gA",)], dma="g")
            while conv_jobs:
                emit_conv()
            sc.add("pool", lambda e: e.dma_start(out=R3[:, 0:8192], in_=wout_sc[:, :]),
                   reads=WSC_KEYS, writes=[("wout",)], dma="wout")
            wup_n = [0]
            wdn_done = [False]

            def load_wup(fp):
                sl = wup_n[0] % 4
                wup_n[0] += 1
                sc.add("pool", lambda e, sl=sl, fp=fp: e.dma_start(out=R3[:, 8192 + sl * 2048: 8192 + (sl + 1) * 2048],
                                                                   in_=wup_sc[:, fp * 2048:(fp + 1) * 2048]),
                       reads=WSC_KEYS, writes=[("wup", sl)], dma="wup%d" % sl)
                if wup_n[0] == 4 and not wdn_done[0]:
                    wdn_done[0] = True
                    for f0 in range(4):
                        sc.add("pool", lambda e, f0=f0: e.dma_start(out=R1[:, f0 * 8192:(f0 + 1) * 8192],
                                                                   in_=wdn_sc[:, f0 * 8192:(f0 + 1) * 8192]),
                               reads=WSC_KEYS, writes=[("wdown",)], dma="wdown")
                return sl
            y_pairs = Rot([0, 1])
            tpC = Rot([4, 5])
            up_banks = Rot([4, 5, 6, 7])
            dn_pairs = Rot([0, 1, 2, 3])
            rsl = Rot([0, 1])
            gscr = [abf[:, i, :].bitcast(F32) for i in range(2)]
            for c in range(NCH):
                pend_tr = None
                for t in range(5):
                    if t == 4:
                        transpose_to_actT(*pend_tr, rot=tpC)
                        break
                    T = 4 * c + t
                    slot = cnt["x"] % 2
                    cnt["x"] += 1
                    load_x(tok0 + T * 128, slot)
                    q = y_pairs.next()
                    for half in range(2):
                        bank = 2 * q + half

                        def f(e, bank=bank, T=T, half=half):
                            ins = None
                            for k in range(8):
                                ins = e.matmul(ps[:, bank, :], lhsT=ACTT[:, k, T * 128:(T + 1) * 128],
                                               rhs=WOUT[:, k, half * 512:(half + 1) * 512], start=(k == 0), stop=(k == 7))
                            return ins
                        sc.add("pe", f, reads=[("act", T), ("wout",)], writes=[("ps", bank)])
                        sc.add("dve", lambda e, bank=bank, t=t, half=half, slot=slot: e.tensor_tensor(
                            out=HB[:, t, half * 512:(half + 1) * 512], in0=ps[:, bank, :],
                            in1=xbuf[:, slot, half * 512:(half + 1) * 512], op=ALU.add),
                            reads=[("ps", bank), ("x", slot)], writes=[("h", t, half)])
                    rms_to_bf16(HB[:, t, :], [("h", t, 0), ("h", t, 1)], gA[:, :], ("gA",), abf[:, slot, :], ("abf", slot),
                                abf[:, slot, :], ("abf", slot))
                    if pend_tr is not None:
                        transpose_to_actT(*pend_tr, rot=tpC)
                    pend_tr = (abf[:, slot, :], ("abf", slot), T)
                mkeys = [("act", 4 * c + t) for t in range(4)]
                for fp in range(16):
                    sl = load_wup(fp)
                    for f4 in range(2):
                        fb = fp * 2 + f4
                        bank = up_banks.next()

                        def f(e, bank=bank, sl=sl, f4=f4, c=c):
                            ins = None
                            for k in range(8):
                                ins = e.matmul(ps[:, bank, :], lhsT=WUP4[sl][:, k, f4 * 128:(f4 + 1) * 128],
                                               rhs=ACTT[:, k, c * 512:(c + 1) * 512], start=(k == 0), stop=(k == 7))
                            return ins
                        sc.add("pe", f, reads=[("wup", sl)] + mkeys, writes=[("ps", bank)])
                        rsi = rsl.next()
                        rbuf = gscr[rsi]
                        sc.add("act", lambda e, bank=bank, rbuf=rbuf: e.activation(out=rbuf, in_=ps[:, bank, :], func=AF.Relu),
                               reads=[("ps", bank)], writes=[("abf", rsi)])
                        sc.add("dve", lambda e, rbuf=rbuf, fb=fb: e.tensor_tensor(out=UT[:, fb, :], in0=rbuf, in1=rbuf, op=ALU.mult),
                               reads=[("abf", rsi)], writes=[("uT", fb)])
                ukeys = [("uT", fb) for fb in range(32)]
                for t in range(4):
                    T = 4 * c + t
                    q = dn_pairs.next()
                    for half in range(2):
                        bank = 2 * q + half

                        def f(e, bank=bank, t=t, half=half):
                            ins = None
                            for fb in range(32):
                                ins = e.matmul(ps[:, bank, :], lhsT=UT[:, fb, t * 128:(t + 1) * 128],
                                               rhs=WDN[:, fb, half * 512:(half + 1) * 512], start=(fb == 0), stop=(fb == 31))
                            return ins
                        sc.add("pe", f, reads=ukeys + [("wdown",)], writes=[("ps", bank)])
                        sc.add("dve", lambda e, bank=bank, t=t, half=half: e.tensor_tensor(
                            out=HB[:, t, half * 512:(half + 1) * 512], in0=ps[:, bank, :],
                            in1=HB[:, t, half * 512:(half + 1) * 512], op=ALU.add),
                            reads=[("ps", bank), ("h", t, half)], writes=[("h", t, half)])
                    slot = cnt["x"] % 2
                    cnt["x"] += 1
                    rms_to_bf16(HB[:, t, :], [("h", t, 0), ("h", t, 1)], gF[:, :], ("gF",), xbuf[:, slot, :], ("x", slot),
                                abf[:, slot, :], ("abf", slot))
                    r0 = tok0 + T * 128
                    sc.add("sp", lambda e, slot=slot, r0=r0: e.dma_start(out=out[r0:r0 + 128, :], in_=xbuf[:, slot, :]),
                           reads=[("x", slot)], dma="o%d" % slot)

          except _Stop:
            break

        sc.emit(nc, block, sems, dma_sems, ["o0", "o1"])
    return nc


_NC_CACHE = {}


def make_in_maps(inputs, n_cores, nseq):
    cst, btab = make_consts()
    f = lambda a: np.ascontiguousarray(np.asarray(a, dtype=np.float32))
    x = f(inputs["x"])
    rep = lambda v: np.ascontiguousarray(np.broadcast_to(f(v).reshape(1, -1), (128, f(v).size)))
    lamv = rep(np.concatenate([f(inputs["lambda_q1"])[0], f(inputs["lambda_k1"])[0],
                               f(inputs["lambda_q2"])[0], f(inputs["lambda_k2"])[0]]))
    gcol = np.ascontiguousarray(np.stack([f(inputs["diff_subln"])[0], np.tile(f(inputs["sb_subln"])[0], 2)], axis=1))
    common = {
        "w_in": f(inputs["w_in"])[0], "w_out": f(inputs["w_out"])[0],
        "w_up": f(inputs["w_up"])[0], "w_down": f(inputs["w_down"])[0],
        "g_attn": rep(inputs["attn_norm"][0]), "g_mlp": rep(inputs["mlp_norm"][0]),
        "g_fin": rep(inputs["final_norm"]), "lamv": lamv, "gcol": gcol, "cst": cst, "btab": btab,
    }
    maps = []
    for c in range(n_cores):
        m = dict(common)
        m["x"] = np.ascontiguousarray(x[c * nseq:(c + 1) * nseq].reshape(nseq * S, D))
        maps.append(m)
    return maps


def kernel(**inputs):
    n_cores = 8
    nseq = 2
    if "nc" not in _NC_CACHE:
        _NC_CACHE["nc"] = build_nc(nseq)
    nc = _NC_CACHE["nc"]
    in_maps = make_in_maps(inputs, n_cores, nseq)
    res = run_bass_kernel_spmd(nc, in_maps, core_ids=list(range(n_cores)))
    outs = [np.asarray(r["out"]).reshape(nseq, S, D) for r in res.results]
    return np.concatenate(outs, axis=0).astype(np.float32)
```

```python
import math
import numpy as np
import ml_dtypes
import concourse.bass as bass
import concourse.mybir as mybir
from concourse.bass_utils import run_bass_kernel_spmd

F32 = mybir.dt.float32
BF16 = mybir.dt.bfloat16
AF = mybir.ActivationFunctionType
ALU = mybir.AluOpType
AX = mybir.AxisListType

D = 1024
S = 2048
NT = S // 128
NCH = S // 512
DFF = 4096
EPS = 1e-6
LAM_INIT = 0.8 - 0.6 * math.exp(0.0)
SLOPES = [2.0 ** (-8.0 * (h + 1) / 4) for h in range(4)]
NEG = -30000.0

C_ID, C_NTRI, C_NONES, C_ONES, C_BONES, C_MD, C_MS = range(7)
CST_W = 7 * 128 + 512 + 512


class Op:
    __slots__ = ("idx", "eng", "fn", "dma", "dma_val", "waits", "signal", "sig")


class Sched:
    ENGS = ("pe", "act", "dve", "pool", "sp")

    def __init__(self):
        self.ops = []
        self.last_w = {}
        self.readers = {}
        self.dma_n = {}
        self.last_eng = {}
        self.last_dma = {}
        self.pending = {e: [] for e in self.ENGS}
        self.extra = {}

    def add(self, eng, fn, reads=(), writes=(), dma=None):
        writes = list(writes) + [k for k in reads if k[0] == "ps"]
        op = Op()
        op.idx = len(self.ops)
        op.eng = eng
        op.fn = fn
        op.dma = dma
        op.dma_val = 0
        op.signal = False
        op.sig = 0
        deps = {}

        def consider(p, kind):
            if p is None:
                return
            if p.dma is None and dma is None and p.eng == eng and kind != "raw":
                return
            key = ("dma", p.dma) if p.dma is not None else p.eng
            cur = deps.get(key)
            if cur is None or cur.idx < p.idx:
                deps[key] = p

        for k in reads:
            consider(self.last_w.get(k), "raw")
        for k in writes:
            consider(self.last_w.get(k), "waw")
            for r in self.readers.get(k, {}).values():
                consider(r, "war")
        for p in self.pending[eng]:
            consider(p, "bar")
        self.pending[eng] = []
        for k in list(reads) + list(writes):
            for p in self.extra.pop(k, ()):
                consider(p, "bar")
        op.waits = list(deps.values())
        for p in op.waits:
            p.signal = True
        if dma is not None:
            self.dma_n[dma] = self.dma_n.get(dma, 0) + 1
            op.dma_val = 16 * self.dma_n[dma]
            self.last_dma[dma] = op
        else:
            self.last_eng[eng] = op
        wset = set(writes)
        for k in wset:
            self.last_w[k] = op
            self.readers[k] = {}
        rk = ("dma", dma) if dma is not None else eng
        for k in reads:
            if k not in wset:
                self.readers.setdefault(k, {})[rk] = op
        self.ops.append(op)
        return op

    def guard(self, new_keys, old_keys):
        ops = []
        for k in old_keys:
            w = self.last_w.get(k)
            if w is not None:
                ops.append(w)
            ops.extend(self.readers.get(k, {}).values())
        for k in new_keys:
            self.extra.setdefault(k, []).extend(ops)

    def barrier(self):
        deps = list(self.last_eng.values()) + list(self.last_dma.values())
        for e in self.ENGS:
            self.pending[e] = list(deps)

    def emit(self, nc, block, sems, dma_sems, final_dma_keys):
        cnt = {e: 0 for e in self.ENGS}
        for op in self.ops:
            if op.dma is None and op.signal:
                cnt[op.eng] += 1
                op.sig = cnt[op.eng]
        per_eng = {e: [o for o in self.ops if o.eng == e] for e in self.ENGS}

        def run(engobj, ename):
            waited = {}
            for op in per_eng[ename]:
                for p in op.waits:
                    if p.dma is not None:
                        sem, val, sk = dma_sems[p.dma], p.dma_val, ("d", p.dma)
                    else:
                        sem, val, sk = sems[p.eng], p.sig, ("e", p.eng)
                    if waited.get(sk, 0) >= val:
                        continue
                    waited[sk] = val
                    engobj.wait_ge(sem, val)
                ins = op.fn(engobj)
                if op.dma is not None:
                    ins.then_inc(dma_sems[op.dma], 16)
                elif op.signal:
                    ins.then_inc(sems[ename], 1)
            if ename == "sp":
                for k in final_dma_keys:
                    if k in self.dma_n:
                        engobj.wait_ge(dma_sems[k], 16 * self.dma_n[k])

        @block.tensor
        def _(e):
            run(e, "pe")

        @block.scalar
        def _(e):
            run(e, "act")

        @block.vector
        def _(e):
            run(e, "dve")

        @block.gpsimd
        def _(e):
            run(e, "pool")

        @block.sync
        def _(e):
            run(e, "sp")


class Rot:
    def __init__(self, items):
        self.items = list(items)
        self.i = 0

    def next(self):
        v = self.items[self.i % len(self.items)]
        self.i += 1
        return v


def make_consts():
    bf = ml_dtypes.bfloat16
    cst = np.zeros((128, CST_W), np.float32)
    i = np.arange(128)[:, None]
    j = np.arange(128)[None, :]
    cst[:, C_ID * 128:(C_ID + 1) * 128] = (i == j)
    cst[:, C_NTRI * 128:(C_NTRI + 1) * 128] = -(i >= j).astype(np.float32)
    cst[:, C_NONES * 128:(C_NONES + 1) * 128] = -1.0
    cst[:, C_ONES * 128:(C_ONES + 1) * 128] = 1.0
    cst[:, C_BONES * 128:(C_BONES + 1) * 128] = ((i // 64) == (j // 64))
    cst[:, C_MD * 128:(C_MD + 1) * 128] = NEG * (i > j)
    cst[:, C_MS * 128:(C_MS + 1) * 128] = NEG * (i >= j)
    o = 7 * 128
    jj = np.arange(512)
    cst[0, o:o + 512] = -(jj - jj % 128)
    cst[1, o:o + 512] = -(jj % 128)
    o2 = o + 512
    for h in range(4):
        cst[0:2, o2 + h * 128:o2 + (h + 1) * 128] = SLOPES[h]
    btab = np.zeros((128, 64), np.float32)
    for h in range(4):
        for d in range(16):
            btab[:, h * 16 + d] = SLOPES[h] * np.arange(128) - SLOPES[h] * 128.0 * (d - 3)
    return cst.astype(bf), btab


class _Stop(Exception):
    pass


def build_nc(nseq=2, stage=99, stage1=99):
    nc = bass.Bass("TRN2", target_bir_lowering=False)
    ntok = nseq * S
    x = nc.dram_tensor("x", [ntok, D], F32, kind="ExternalInput").ap()
    w_in = nc.dram_tensor("w_in", [D, 3 * D], F32, kind="ExternalInput").ap()
    w_out = nc.dram_tensor("w_out", [D, D], F32, kind="ExternalInput").ap()
    w_up = nc.dram_tensor("w_up", [D, DFF], F32, kind="ExternalInput").ap()
    w_down = nc.dram_tensor("w_down", [DFF, D], F32, kind="ExternalInput").ap()
    g_attn_d = nc.dram_tensor("g_attn", [128, D], F32, kind="ExternalInput").ap()
    g_mlp_d = nc.dram_tensor("g_mlp", [128, D], F32, kind="ExternalInput").ap()
    g_fin_d = nc.dram_tensor("g_fin", [128, D], F32, kind="ExternalInput").ap()
    lamv_d = nc.dram_tensor("lamv", [128, 256], F32, kind="ExternalInput").ap()
    gcol_d = nc.dram_tensor("gcol", [128, 2], F32, kind="ExternalInput").ap()
    cst_d = nc.dram_tensor("cst", [128, CST_W], BF16, kind="ExternalInput").ap()
    btab_d = nc.dram_tensor("btab", [128, 64], F32, kind="ExternalInput").ap()
    out = nc.dram_tensor("out", [ntok, D], F32, kind="ExternalOutput").ap()

    wup_sc = nc.dram_tensor("wup_sc", [128, 16 * 8 * 256], BF16).ap()
    wdn_sc = nc.dram_tensor("wdn_sc", [128, 32 * 1024], BF16).ap()
    wout_sc = nc.dram_tensor("wout_sc", [128, 8 * 1024], BF16).ap()
    win_sc = nc.dram_tensor("win_sc", [128, 8 * 3072], BF16).ap()
    w_in_v = w_in.rearrange("(k p) e -> p k e", p=128)
    w_out_v = w_out.rearrange("(k p) e -> p k e", p=128)
    w_up_v = w_up.rearrange("(k p) e -> p k e", p=128)
    w_down_v = w_down.rearrange("(k p) e -> p k e", p=128)

    sc = Sched()

    from contextlib import ExitStack
    with ExitStack() as es_:
        en = es_.enter_context
        R1 = en(nc.sbuf_tensor("R1", [128, 32768], BF16))
        R2 = en(nc.sbuf_tensor("R2", [128, 16384], BF16))
        R3 = en(nc.sbuf_tensor("R3", [128, 24576], BF16))
        R4 = en(nc.sbuf_tensor("R4", [128, 16384], BF16))
        xbuf = en(nc.sbuf_tensor("xbuf", [128, 2, D], F32))
        abf = en(nc.sbuf_tensor("abf", [128, 2, D], BF16))
        gA = en(nc.sbuf_tensor("gA", [128, D], F32))
        gF = en(nc.sbuf_tensor("gF", [128, D], F32))
        cst = en(nc.sbuf_tensor("cst_sb", [128, CST_W], BF16))
        btab = en(nc.sbuf_tensor("btab_sb", [128, 64], F32))
        lamv = en(nc.sbuf_tensor("lamv_sb", [128, 256], F32))
        small = en(nc.sbuf_tensor("small", [128, 64], F32))
        stg = en(nc.sbuf_tensor("stg", [128, 2, 1024], BF16))
        ps = en(nc.psum_tensor("ps", [128, 8, 512], F32))
        s_pe = en(nc.semaphore("s_pe"))
        s_act = en(nc.semaphore("s_act"))
        s_dve = en(nc.semaphore("s_dve"))
        s_pool = en(nc.semaphore("s_pool"))
        s_sp = en(nc.semaphore("s_sp"))
        dma_sems = {}
        for nm in ("x0", "x1", "o0", "o1", "win", "wout", "wdown", "wup0", "wup1", "wup2", "wup3", "cst", "g", "cv0", "cv1", "cs0", "cs1", "wc0", "wc1", "wc2", "wc3", "wc4", "wc5"):
            dma_sems[nm] = en(nc.semaphore("d_" + nm))
        block = en(nc.Block())
        sems = {"pe": s_pe, "act": s_act, "dve": s_dve, "pool": s_pool, "sp": s_sp}

        QT = R1[:, :].rearrange("p (b t) -> p b t", b=16)
        WDN = R1[:, :].rearrange("p (f d) -> p f d", f=32)
        V = R2[:, :].rearrange("p (t e) -> p t e", t=16)
        UT = R2[:, :].rearrange("p (f t) -> p f t", f=32)
        WIN = R3[:, :].rearrange("p (k e) -> p k e", k=8)
        WOUT = R3[:, 0:8192].rearrange("p (k e) -> p k e", k=8)
        WUP = [R3[:, 8192 + i * 4096: 8192 + (i + 1) * 4096].rearrange("p (k e) -> p k e", k=8)
               for i in range(2)]
        WUP4 = [R3[:, 8192 + i * 2048: 8192 + (i + 1) * 2048].rearrange("p (k e) -> p k e", k=8)
                for i in range(4)]
        HB = R3[:, 16384:24576].bitcast(F32).rearrange("p (t d) -> p t d", t=4)
        ACTT = R4[:, :].rearrange("p (k t) -> p k t", k=8)

        KP = [xbuf[:, i, :].bitcast(BF16) for i in range(2)]
        VP = [abf[:, :, :].rearrange("p a b -> p (a b)").rearrange("p (t e) -> p t e", e=128),
              gA[:, :].bitcast(BF16).rearrange("p (t e) -> p t e", e=128)]
        qaugP = cst[:, 7 * 128: 7 * 128 + 512]
        kaugP = [cst[:, 7 * 128 + 512 + h * 128: 7 * 128 + 512 + (h + 1) * 128] for h in range(4)]
        off = [0]

        def carve(nel, dt=BF16, shape=None):
            n16 = nel if dt == BF16 else nel * 2
            a = R3[:, off[0]:off[0] + n16]
            off[0] += n16
            if dt == F32:
                a = a.bitcast(F32)
            if shape is not None:
                a = a.rearrange("p (a b) -> p a b", a=shape[0])
            return a

        Pb = [carve(1024, BF16, (2, 512)) for _ in range(2)]
        LN1 = [carve(512, F32) for _ in range(2)]
        LN2 = [carve(512, F32) for _ in range(2)]
        O1S = [carve(512, F32) for _ in range(2)]
        O2S = [carve(512, F32) for _ in range(2)]
        OSQ = [carve(512, BF16) for _ in range(2)]
        RINV = [carve(512, F32) for _ in range(2)]
        Eb = [carve(1024, F32, (2, 512)) for _ in range(2)]
        Lb = [carve(1024, BF16, (2, 512)) for _ in range(3)]
        Wb = [carve(1024, BF16, (2, 512)) for _ in range(2)]
        Rs = [carve(1024, BF16, (2, 512)) for _ in range(2)]
        assert off[0] <= 24576

        def cblk(i):
            return cst[:, i * 128:(i + 1) * 128]

        ident = cblk(C_ID)
        ntri = cblk(C_NTRI)
        nones = cblk(C_NONES)
        ones = cblk(C_ONES)
        bones = cblk(C_BONES)
        maskd = cblk(C_MD)
        masks = cblk(C_MS)
        qaug = cst[0:2, 7 * 128: 7 * 128 + 512]
        kaug = [cst[0:2, 7 * 128 + 512 + h * 128: 7 * 128 + 512 + (h + 1) * 128] for h in range(4)]

        SS = small[:, 0:8]
        RSTD = small[:, 8:16]
        LAM = small[:, 16:24]
        GCOL = small[:, 24:26]
        LTMP = small[:, 32:64]
        lamtmp = lamv

        sc.add("sp", lambda e: e.dma_start(out=cst[:, :], in_=cst_d), writes=[("cst",)], dma="cst")
        sc.add("sp", lambda e: e.dma_start(out=btab[:, :], in_=btab_d), writes=[("cst2",)], dma="cst")
        sc.add("sp", lambda e: e.dma_start(out=lamv[:, :], in_=lamv_d), writes=[("lamv",)], dma="cst")
        sc.add("sp", lambda e: e.dma_start(out=GCOL, in_=gcol_d), writes=[("gcol",)], dma="cst")
        sc.add("sp", lambda e: e.dma_start(out=gF[:, :], in_=g_fin_d), writes=[("gF",)], dma="cst")
        for op_ in sc.ops:
            if op_.dma == "cst":
                op_.dma_val = 16 * sc.dma_n["cst"]
        sc.add("dve", lambda e: e.tensor_tensor(out=lamv[:, 0:64], in0=lamv[:, 0:64], in1=lamv[:, 64:128], op=ALU.mult),
               reads=[("lamv",)], writes=[("lamv",)])
        sc.add("dve", lambda e: e.tensor_tensor(out=lamv[:, 128:192], in0=lamv[:, 128:192], in1=lamv[:, 192:256], op=ALU.mult),
               reads=[("lamv",)], writes=[("lamv",)])
        sc.add("dve", lambda e: e.reduce_sum(out=LAM[:, 0:1], in_=lamv[:, 0:64], axis=AX.X),
               reads=[("lamv",)], writes=[("lam0",)])
        sc.add("dve", lambda e: e.reduce_sum(out=LAM[:, 1:2], in_=lamv[:, 128:192], axis=AX.X),
               reads=[("lamv",)], writes=[("lam1",)])
        sc.add("act", lambda e: e.activation(out=LAM[:, 2:4], in_=LAM[:, 0:2], func=AF.Exp),
               reads=[("lam0",), ("lam1",)], writes=[("lam2",)])
        sc.add("dve", lambda e: e.tensor_tensor(out=LAM[:, 4:5], in0=LAM[:, 3:4], in1=LAM[:, 2:3], op=ALU.subtract),
               reads=[("lam2",)], writes=[("lam4",)])
        sc.add("dve", lambda e: e.tensor_scalar(out=LAM[:, 5:6], in0=LAM[:, 4:5], scalar1=-LAM_INIT, scalar2=None, op0=ALU.add),
               reads=[("lam4",)], writes=[("neglam",)])
        sc.add("dve", lambda e: e.tensor_scalar(out=GCOL[:, 0:1], in0=GCOL[:, 0:1], scalar1=(1.0 - LAM_INIT) * math.sqrt(128.0), scalar2=None, op0=ALU.mult),
               reads=[("gcol",)], writes=[("gcol",)])
        sc.add("dve", lambda e: e.tensor_scalar(out=GCOL[:, 1:2], in0=GCOL[:, 1:2], scalar1=8.0, scalar2=None, op0=ALU.mult),
               reads=[("gcol",)], writes=[("gcol",)])
        NEGLAM = LAM[:, 5:6]

        cnt = {"ss": 0, "x": 0, "ev": 0}

        def rms_to_bf16(xs_ap, xkey, gain_ap, gkey, dst_ap, dkey, junk_ap, junkkey):
            i = cnt["ss"] % 8
            cnt["ss"] += 1
            sc.add("act", lambda e: e.activation(out=junk_ap, in_=xs_ap, func=AF.Square, accum_out=SS[:, i:i + 1]),
                   reads=list(xkey), writes=[junkkey, ("ss", i)])
            sc.add("dve", lambda e: e.tensor_scalar(out=RSTD[:, i:i + 1], in0=SS[:, i:i + 1], scalar1=1.0 / D, scalar2=EPS,
                                                    op0=ALU.mult, op1=ALU.add),
                   reads=[("ss", i)], writes=[("rstd", i)])
            sc.add("act", lambda e: e.activation(out=RSTD[:, i:i + 1], in_=RSTD[:, i:i + 1], func=AF.Ln),
                   reads=[("rstd", i)], writes=[("rstd", i)])
            sc.add("act", lambda e: e.activation(out=RSTD[:, i:i + 1], in_=RSTD[:, i:i + 1], func=AF.Exp, scale=-0.5),
                   reads=[("rstd", i)], writes=[("rstd", i)])
            sc.add("dve", lambda e: e.scalar_tensor_tensor(out=dst_ap, in0=xs_ap, scalar=RSTD[:, i:i + 1], in1=gain_ap,
                                                           op0=ALU.mult, op1=ALU.mult),
                   reads=list(xkey) + [("rstd", i), gkey], writes=[dkey])

        tp_banks = Rot([0, 1])

        def transpose_to_actT(src_bf, srckey, T, rot=None):
            b = (rot or tp_banks).next()
            tpv = ps[:, b, :].bitcast(BF16).rearrange("p (k t) -> p k t", k=8)

            def f(e):
                ins = None
                for k in range(8):
                    ins = e.transpose(out=tpv[:, k, :], in_=src_bf[:, k * 128:(k + 1) * 128], identity=ident)
                return ins
            sc.add("pe", f, reads=[srckey, ("cst",)], writes=[("ps", b)])
            dst = ACTT[:, :, T * 128:(T + 1) * 128]
            if cnt["ev"] % 2 == 0:
                sc.add("act", lambda e: e.activation(out=dst, in_=tpv, func=AF.Copy), reads=[("ps", b)], writes=[("act", T)])
            else:
                sc.add("dve", lambda e: e.tensor_copy(out=dst, in_=tpv), reads=[("ps", b)], writes=[("act", T)])
            cnt["ev"] += 1

        def evac(dst, dkeys, bank, scale=1.0):
            src = ps[:, bank, :]
            if cnt["ev"] % 2 == 0:
                sc.add("act", lambda e: e.activation(out=dst, in_=src, func=AF.Copy, scale=scale),
                       reads=[("ps", bank)], writes=dkeys)
            else:
                sc.add("dve", lambda e: e.tensor_scalar(out=dst, in0=src, scalar1=scale, scalar2=None, op0=ALU.mult),
                       reads=[("ps", bank)], writes=dkeys)
            cnt["ev"] += 1

        def load_x(row0, slot):
            sc.add("sp", lambda e: e.dma_start(out=xbuf[:, slot, :], in_=x[row0:row0 + 128, :]),
                   writes=[("x", slot)], dma="x%d" % slot)

        conv_jobs = [("out", k, 0) for k in range(8)] + [("up", k, j) for j in range(8) for k in range(8)] \
            + [("dn", fb, 0) for fb in range(32)] + ([("in", k, j) for k in range(8) for j in range(3)] if nseq > 1 else [])
        conv_n = [0]
        WSC_KEYS = [("wsc", nm, sl) for nm in ("up", "dn", "out", "in") for sl in range(2)]

        def emit_conv():
            if not conv_jobs:
                return
            kind, a, b = conv_jobs.pop(0)
            sl = conv_n[0] % 2
            conv_n[0] += 1
            if kind == "up":
                k, j = a, b
                sc.add("pool", lambda e: e.dma_start(out=stg[:, sl, 0:512], in_=w_up_v[:, k, j * 512:(j + 1) * 512]),
                       writes=[("stg", sl)], dma="cv%d" % sl)
                for hh in range(2):
                    o = ((2 * j + hh) * 8 + k) * 256
                    sc.add("sp", lambda e, o=o, hh=hh: e.dma_start(out=wup_sc[:, o:o + 256], in_=stg[:, sl, hh * 256:(hh + 1) * 256]),
                           reads=[("stg", sl)], writes=[("wsc", "up", sl)], dma="cs%d" % sl)
            elif kind == "in":
                k, j = a, b
                sc.add("pool", lambda e: e.dma_start(out=stg[:, sl, :], in_=w_in_v[:, k, j * 1024:(j + 1) * 1024]),
                       writes=[("stg", sl)], dma="cv%d" % sl)
                o = k * 3072 + j * 1024
                sc.add("sp", lambda e: e.dma_start(out=win_sc[:, o:o + 1024], in_=stg[:, sl, :]),
                       reads=[("stg", sl)], writes=[("wsc", "in", sl)], dma="cs%d" % sl)
            elif kind == "dn":
                fb = a
                sc.add("pool", lambda e: e.dma_start(out=stg[:, sl, :], in_=w_down_v[:, fb, :]),
                       writes=[("stg", sl)], dma="cv%d" % sl)
                sc.add("sp", lambda e: e.dma_start(out=wdn_sc[:, fb * 1024:(fb + 1) * 1024], in_=stg[:, sl, :]),
                       reads=[("stg", sl)], writes=[("wsc", "dn", sl)], dma="cs%d" % sl)
            else:
                k = a
                sc.add("pool", lambda e: e.dma_start(out=stg[:, sl, :], in_=w_out_v[:, k, :]),
                       writes=[("stg", sl)], dma="cv%d" % sl)
                sc.add("sp", lambda e: e.dma_start(out=wout_sc[:, k * 1024:(k + 1) * 1024], in_=stg[:, sl, :]),
                       reads=[("stg", sl)], writes=[("wsc", "out", sl)], dma="cs%d" % sl)

        QT_KEYS = [("QT", b_, c_) for b_ in range(16) for c_ in range(NCH)]
        V_KEYS = [("V", t_, h_) for t_ in range(NT) for h_ in range(2)]
        WORK_KEYS = ([("P", i_) for i_ in range(2)] + [(n_, i_) for n_ in ("ln1", "ln2", "o1s", "o2s", "osq", "rinv", "E", "W", "R")
                                                      for i_ in range(2)] + [("L", i_) for i_ in range(3)])
        CBUF_KEYS = [("wout",)] + [("wup", i_) for i_ in range(4)] + [("h", t_, h_) for t_ in range(4) for h_ in range(2)]
        UT_KEYS = [("uT", f_) for f_ in range(32)]

        for s in range(nseq):
          try:
            tok0 = s * S
            if s >= 1:
                stage = stage1
            if stage < 1:
                raise _Stop()
            sc.guard(QT_KEYS, [("wdown",)])
            sc.guard(V_KEYS, UT_KEYS)
            sc.guard([("win", 0)], [("wout",)])
            sc.guard([("win", 1)], [("wup", i_) for i_ in range(4)])
            sc.guard([("win", 2)], [("h", t_, h_) for t_ in range(4) for h_ in range(2)])
            sc.add("sp", lambda e: e.dma_start(out=gA[:, :], in_=g_attn_d), writes=[("gA",)], dma="g")
            if s == 0:
                for k in range(8):
                    sc.add("pool", lambda e, k=k: e.dma_start(out=WIN[:, k, :], in_=w_in_v[:, k, :]),
                           writes=[("win", 0), ("win", 1), ("win", 2)], dma="win")
                WINK = lambda col: [("win", 0), ("win", 1), ("win", 2)]
            else:
                for i3 in range(3):
                    sc.add("pool", lambda e, i3=i3: e.dma_start(out=R3[:, i3 * 8192:(i3 + 1) * 8192],
                                                                in_=win_sc[:, i3 * 8192:(i3 + 1) * 8192]),
                           reads=WSC_KEYS, writes=[("win", i3)], dma="win")
                WINK = lambda col: [("win", 0), ("win", 1), ("win", 2)]
            pj_banks = Rot([2, 3, 4, 5, 6, 7])

            def a_chain(c, t):
                slot = cnt["x"] % 2
                cnt["x"] += 1
                load_x(tok0 + (4 * c + t) * 128, slot)
                rms_to_bf16(xbuf[:, slot, :], [("x", slot)], gA[:, :], ("gA",), abf[:, slot, :], ("abf", slot),
                            abf[:, slot, :], ("abf", slot))
                return (abf[:, slot, :], ("abf", slot), 4 * (c % 2) + t)

            def a_groups(c):
                j = c % 2
                akeys = [("act", 4 * j + t) for t in range(4)]
                AT = ACTT[:, :, j * 512:(j + 1) * 512]
                groups = []
                fm = []
                for b in range(4):
                    fm.append((128 * b, b, 0.125))
                for b in range(4):
                    fm.append((512 + 128 * b, 4 + b, 1.0))
                for b in range(4):
                    fm.append((1536 + 128 * b, 8 + b, 0.125))
                for b in range(4):
                    fm.append((2048 + 128 * b, 12 + b, 1.0))
                for (col, blk, scl) in fm:
                    def g(col=col, blk=blk, scl=scl):
                        bank = pj_banks.next()

                        def f(e):
                            ins = None
                            for k in range(8):
                                ins = e.matmul(ps[:, bank, :], lhsT=WIN[:, k, col:col + 128], rhs=AT[:, k, :],
                                               start=(k == 0), stop=(k == 7))
                            return ins
                        sc.add("pe", f, reads=WINK(col) + akeys, writes=[("ps", bank)])
                        evac(QT[:, blk, c * 512:(c + 1) * 512], [("QT", blk, c)], bank, scl)
                    groups.append(g)
                for t in range(4):
                    for half, col in ((0, 1024), (1, 2560)):
                        def g(t=t, half=half, col=col):
                            T = 4 * c + t
                            bank = pj_banks.next()

                            def f(e):
                                ins = None
                                for k in range(8):
                                    ins = e.matmul(ps[:, bank, :], lhsT=AT[:, k, t * 128:(t + 1) * 128],
                                                   rhs=WIN[:, k, col:col + 512], start=(k == 0), stop=(k == 7))
                                return ins
                            sc.add("pe", f, reads=WINK(col) + akeys, writes=[("ps", bank)])
                            evac(V[:, T, half * 512:(half + 1) * 512], [("V", T, half)], bank)
                        groups.append(g)
                return groups

            for t in range(4):
                transpose_to_actT(*a_chain(0, t))
            for c in range(NCH):
                groups = a_groups(c)
                pend_a = None
                for gi_, g in enumerate(groups):
                    if c + 1 < NCH and gi_ % 6 == 0:
                        new_a = a_chain(c + 1, gi_ // 6)
                        if pend_a is not None:
                            transpose_to_actT(*pend_a)
                        pend_a = new_a
                    g()
                if pend_a is not None:
                    transpose_to_actT(*pend_a)

            if stage < 2:
                raise _Stop()
            sc.guard(WORK_KEYS, [("win", 0), ("win", 1), ("win", 2)] + [("winc", cb_) for cb_ in range(6)])
            sc.add("dve", lambda e: e.memset(KP[0][64:128, :], 0.0), writes=[("x", 0)])
            sc.add("dve", lambda e: e.memset(KP[1][0:64, :], 0.0), writes=[("x", 1)])
            sc.add("dve", lambda e: e.memset(VP[0][:, :, 64:128], 0.0), writes=[("abf", 0), ("abf", 1)])
            sc.add("dve", lambda e: e.memset(VP[1][:, :, 0:64], 0.0), writes=[("gA",)])

            def load_kpad(blk):
                sc.add("dve", lambda e: e.tensor_copy(out=KP[0][0:64, :], in_=QT[0:64, blk, :]),
                       reads=[("QT", blk, cc) for cc in range(NCH)], writes=[("x", 0)])
                sc.add("dve", lambda e: e.tensor_copy(out=KP[1][64:128, :], in_=QT[64:128, blk, :]),
                       reads=[("QT", blk, cc) for cc in range(NCH)], writes=[("x", 1)])
            deferred = []

            def defer(it, fn):
                deferred.append((it, fn))

            def run_deferred(it):
                keep = []
                for (d, fn) in deferred:
                    if d <= it:
                        fn()
                    else:
                        keep.append((d, fn))
                deferred[:] = keep

            dunits = [(h, c, kb) for h in range(4) for c in range(NCH) for kb in range(4 * c + 4)]
            nd = len(dunits)
            grp = [0]

            def d_S(i):
                h, c, kb = dunits[i]
                r = max(0, kb - 4 * c)
                col0 = 128 * r
                q = i % 2
                diag = kb >= 4 * c
                if i == 0 or dunits[i - 1][0] != h:
                    load_kpad(4 + h)
                for sm in range(2):
                    bank = 2 * q + sm

                    def f(e, bank=bank, sm=sm, kb=kb, c=c, col0=col0, diag=diag, h=h):
                        ins = e.matmul(ps[:, bank, col0:512], lhsT=KP[sm][:, kb * 128:(kb + 1) * 128],
                                       rhs=QT[:, h, c * 512 + col0:(c + 1) * 512], start=True,
                                       stop=(h > 0 and not diag))
                        if h == 0:
                            ins = e.matmul(ps[:, bank, col0:512], lhsT=kaugP[h], rhs=qaugP[:, col0:512],
                                           start=False, stop=(not diag))
                        if diag:
                            ins = e.matmul(ps[:, bank, col0:col0 + 128], lhsT=ident, rhs=maskd,
                                           start=False, stop=True)
                        return ins
                    sc.add("pe", f, reads=[("x", sm), ("QT", h, c), ("cst",)], writes=[("ps", bank)])

            def d_exp(i):
                h, c, kb = dunits[i]
                col0 = 128 * max(0, kb - 4 * c)
                q = i % 2
                sl = i % 2
                bidx = h * 16 + (4 * c - kb + 3)
                sc.add("act", lambda e: e.activation(
                    out=Pb[sl][:, :, col0:512], in_=ps[:, 2 * q:2 * q + 2, col0:512], func=AF.Exp,
                    bias=btab[:, bidx:bidx + 1]),
                    reads=[("ps", 2 * q), ("ps", 2 * q + 1), ("cst2",)], writes=[("P", sl)])

            def d_AV(i, it):
                h, c, kb = dunits[i]
                col0 = 128 * max(0, kb - 4 * c)
                nkb = 4 * c + 4
                sl = i % 2

                def f(e):
                    st = (kb == 0)
                    sp_ = (kb == nkb - 1)
                    e.matmul(ps[:, 4, col0:512], lhsT=V[:, kb, h * 128:(h + 1) * 128], rhs=Pb[sl][:, 0, col0:512],
                             start=st, stop=sp_)
                    e.matmul(ps[:, 6, col0:512], lhsT=ones, rhs=Pb[sl][:, 0, col0:512], start=st, stop=sp_)
                    e.matmul(ps[:, 5, col0:512], lhsT=V[:, kb, h * 128:(h + 1) * 128], rhs=Pb[sl][:, 1, col0:512],
                             start=st, stop=sp_)
                    return e.matmul(ps[:, 7, col0:512], lhsT=ones, rhs=Pb[sl][:, 1, col0:512], start=st, stop=sp_)
                sc.add("pe", f, reads=[("P", sl), ("V", kb, 0), ("cst",)],
                       writes=[("ps", 4), ("ps", 5), ("ps", 6), ("ps", 7)])
                if kb == nkb - 1:
                    d_norm(h, c, it)

            def d_norm(h, c, it):
                m = grp[0] % 2
                grp[0] += 1
                sc.add("act", lambda e: e.activation(out=LN1[m], in_=ps[:, 6, :], func=AF.Ln), reads=[("ps", 6)], writes=[("ln1", m)])
                sc.add("act", lambda e: e.activation(out=LN2[m], in_=ps[:, 7, :], func=AF.Ln), reads=[("ps", 7)], writes=[("ln2", m)])
                sc.add("dve", lambda e: e.tensor_copy(out=O1S[m], in_=ps[:, 4, :]), reads=[("ps", 4)], writes=[("o1s", m)])
                sc.add("dve", lambda e: e.tensor_copy(out=O2S[m], in_=ps[:, 5, :]), reads=[("ps", 5)], writes=[("o2s", m)])

                def n2():
                    sc.add("act", lambda e: e.activation(out=LN1[m], in_=LN1[m], func=AF.Exp, scale=-1.0),
                           reads=[("ln1", m)], writes=[("ln1", m)])
                    sc.add("act", lambda e: e.activation(out=LN2[m], in_=LN2[m], func=AF.Exp, scale=-1.0),
                           reads=[("ln2", m)], writes=[("ln2", m)])

                def n3():
                    sc.add("dve", lambda e: e.tensor_tensor(out=O1S[m], in0=O1S[m], in1=LN1[m], op=ALU.mult),
                           reads=[("o1s", m), ("ln1", m)], writes=[("o1s", m)])
                    sc.add("dve", lambda e: e.tensor_tensor(out=O2S[m], in0=O2S[m], in1=LN2[m], op=ALU.mult),
                           reads=[("o2s", m), ("ln2", m)], writes=[("o2s", m)])
                    sc.add("dve", lambda e: e.scalar_tensor_tensor(out=O1S[m], in0=O2S[m], scalar=NEGLAM, in1=O1S[m],
                                                                   op0=ALU.mult, op1=ALU.add),
                           reads=[("o1s", m), ("o2s", m), ("neglam",)], writes=[("o1s", m)])

                def n4():
                    sc.add("dve", lambda e: e.tensor_tensor(out=OSQ[m], in0=O1S[m], in1=O1S[m], op=ALU.mult),
                           reads=[("o1s", m)], writes=[("osq", m)])

                def n5(it5):
                    q = (it5 + 1) % 2
                    sc.add("pe", lambda e: e.matmul(ps[:, 2 * q, :], lhsT=ones, rhs=OSQ[m], start=True, stop=True),
                           reads=[("osq", m), ("cst",)], writes=[("ps", 2 * q), ("ps", 2 * q + 1)])
                    sc.add("dve", lambda e: e.tensor_scalar(out=RINV[m], in0=ps[:, 2 * q, :], scalar1=128.0 * EPS,
                                                            scalar2=None, op0=ALU.add),
                           reads=[("ps", 2 * q)], writes=[("rinv", m)])

                def n6():
                    sc.add("act", lambda e: e.activation(out=RINV[m], in_=RINV[m], func=AF.Ln),
                           reads=[("rinv", m)], writes=[("rinv", m)])
                    sc.add("act", lambda e: e.activation(out=RINV[m], in_=RINV[m], func=AF.Exp, scale=-0.5),
                           reads=[("rinv", m)], writes=[("rinv", m)])

                def n7():
                    sc.add("dve", lambda e: e.scalar_tensor_tensor(
                        out=ACTT[:, h, c * 512:(c + 1) * 512], in0=O1S[m], scalar=GCOL[:, 0:1], in1=RINV[m],
                        op0=ALU.mult, op1=ALU.mult),
                        reads=[("o1s", m), ("rinv", m), ("gcol",)], writes=[("act", 4 * c + t) for t in range(4)])
                defer(it + 1, n2)
                defer(it + 2, n3)
                defer(it + 3, n4)
                defer(it + 4, lambda: n5(it + 4))
                defer(it + 5, n6)
                defer(it + 6, n7)

            for it in range(nd + 8):
                emit_conv()
                if it < nd:
                    d_S(it)
                    d_exp(it)
                if 1 <= it <= nd:
                    d_AV(it - 1, it)
                run_deferred(it)
            assert not deferred

            if stage < 3:
                raise _Stop()
            sunits = []
            for p in range(4):
                for c in range(NCH):
                    nkb = 4 * c + 4
                    for ui, kb in enumerate(range(nkb - 1, -1, -1)):
                        sunits.append((c, p, kb, ui, len(sunits) - ui))
            ns = len(sunits)
            sgrp = {}
            gi = 0
            for i, (c, p, kb, ui, first) in enumerate(sunits):
                if ui == 0:
                    sgrp[first] = gi
                    gi += 1

            def s_ginfo(i):
                c, p, kb, ui, first = sunits[i]
                g = sgrp[first]
                return g % 2, 6 + (g % 2)

            def s_z(i):
                c, p, kb, ui, first = sunits[i]
                col0 = 128 * max(0, kb - 4 * c)
                diag = kb >= 4 * c
                q = i % 3
                rs, ab = s_ginfo(i)
                if i == 0 or sunits[i - 1][1] != p:
                    load_kpad(12 + p)
                if ui == 0:
                    sc.add("dve", lambda e: e.memset(Rs[rs], 0.0), writes=[("R", rs)])
                for e_ in range(2):
                    bank = 2 * q + e_

                    def f(e, bank=bank, e_=e_):
                        ins = e.matmul(ps[:, bank, col0:512], lhsT=KP[e_][:, kb * 128:(kb + 1) * 128],
                                       rhs=QT[:, 8 + p, c * 512 + col0:(c + 1) * 512], start=True, stop=False,
                                       skip_group_check=True)
                        if diag:
                            ins = e.matmul(ps[:, bank, col0:col0 + 128], lhsT=ident, rhs=masks,
                                           start=False, stop=False, skip_group_check=True)
                        return ins
                    sc.add("pe", f, reads=[("x", e_), ("QT", 8 + p, c), ("cst",)], writes=[("ps", bank)])

            def s_EL(i):
                c, p, kb, ui, first = sunits[i]
                col0 = 128 * max(0, kb - 4 * c)
                q = i % 3
                es = i % 2
                ls = i % 3
                zkeys = [("ps", 2 * q), ("ps", 2 * q + 1)]
                sc.add("act", lambda e: e.activation(
                    out=Eb[es][:, :, col0:512], in_=ps[:, 2 * q:2 * q + 2, col0:512], func=AF.Exp),
                    reads=zkeys, writes=[("E", es)])
                sc.add("act", lambda e: e.activation(
                    out=Lb[ls][:, :, col0:512], in_=Eb[es][:, :, col0:512], func=AF.Ln, bias=1.0),
                    reads=[("E", es)], writes=[("L", ls)])

            def s_tri(i):
                c, p, kb, ui, first = sunits[i]
                col0 = 128 * max(0, kb - 4 * c)
                q = i % 3
                ls = i % 3
                ws = i % 2
                rs, ab = s_ginfo(i)
                for e_ in range(2):
                    bank = 2 * q + e_

                    def f(e, bank=bank, e_=e_):
                        ins = e.matmul(ps[:, bank, col0:512], lhsT=ntri, rhs=Lb[ls][:, e_, col0:512],
                                       start=False, stop=(ui == 0), skip_group_check=True)
                        if ui > 0:
                            ins = e.matmul(ps[:, bank, col0:512], lhsT=nones, rhs=Rs[rs][:, e_, col0:512],
                                           start=False, stop=True, skip_group_check=True)
                        return ins
                    sc.add("pe", f, reads=[("L", ls), ("R", rs), ("cst",)], writes=[("ps", bank)])
                zkeys = [("ps", 2 * q), ("ps", 2 * q + 1)]
                sc.add("act", lambda e: e.activation(
                    out=Wb[ws][:, :, col0:512], in_=ps[:, 2 * q:2 * q + 2, col0:512], func=AF.Exp),
                    reads=zkeys, writes=[("W", ws)])
                if kb > 0:
                    sc.add("dve", lambda e: e.tensor_tensor(
                        out=Rs[rs][:, :, col0:512], in0=Rs[rs][:, :, col0:512], in1=Lb[ls][:, :, col0:512], op=ALU.add),
                        reads=[("R", rs), ("L", ls)], writes=[("R", rs)])

            def s_AV(i, it):
                c, p, kb, ui, first = sunits[i]
                col0 = 128 * max(0, kb - 4 * c)
                ws = i % 2
                rs, ab = s_ginfo(i)
                if i == 0 or sunits[i - 1][1] != p:
                    vkeys = [("V", T_, 1) for T_ in range(NT)]
                    sc.add("dve", lambda e: e.tensor_copy(out=VP[0][:, :, 0:64], in_=V[:, :, 512 + 128 * p:512 + 128 * p + 64]),
                           reads=vkeys, writes=[("abf", 0), ("abf", 1)])
                    sc.add("dve", lambda e: e.tensor_copy(out=VP[1][:, :, 64:128], in_=V[:, :, 512 + 128 * p + 64:512 + 128 * p + 128]),
                           reads=vkeys, writes=[("gA",)])

                def f(e):
                    ins = None
                    for e_ in range(2):
                        ins = e.matmul(ps[:, ab, col0:512], lhsT=VP[e_][:, kb, :],
                                       rhs=Wb[ws][:, e_, col0:512], start=(ui == 0 and e_ == 0),
                                       stop=(kb == 0 and e_ == 1), skip_group_check=True)
                    return ins
                sc.add("pe", f, reads=[("W", ws), ("abf", 0), ("abf", 1), ("gA",)], writes=[("ps", ab)])
                if kb == 0:
                    s_norm(c, p, ab, it)

            def s_norm(c, p, ab, it):
                m = grp[0] % 2
                grp[0] += 1
                sc.add("dve", lambda e: e.tensor_copy(out=O1S[m], in_=ps[:, ab, :]), reads=[("ps", ab)], writes=[("o1s", m)])
                sc.add("dve", lambda e: e.tensor_tensor(out=OSQ[m], in0=O1S[m], in1=O1S[m], op=ALU.mult),
                       reads=[("o1s", m)], writes=[("osq", m)])

                def m2(it2):
                    q = (it2 + 1) % 3
                    sc.add("pe", lambda e: e.matmul(ps[:, 2 * q, :], lhsT=bones, rhs=OSQ[m], start=True, stop=True),
                           reads=[("osq", m), ("cst",)], writes=[("ps", 2 * q), ("ps", 2 * q + 1)])
                    sc.add("dve", lambda e: e.tensor_scalar(out=RINV[m], in0=ps[:, 2 * q, :], scalar1=64.0 * EPS,
                                                            scalar2=None, op0=ALU.add),
                           reads=[("ps", 2 * q)], writes=[("rinv", m)])

                def m3():
                    sc.add("act", lambda e: e.activation(out=RINV[m], in_=RINV[m], func=AF.Ln),
                           reads=[("rinv", m)], writes=[("rinv", m)])
                    sc.add("act", lambda e: e.activation(out=RINV[m], in_=RINV[m], func=AF.Exp, scale=-0.5),
                           reads=[("rinv", m)], writes=[("rinv", m)])

                def m4():
                    sc.add("dve", lambda e: e.scalar_tensor_tensor(
                        out=ACTT[:, 4 + p, c * 512:(c + 1) * 512], in0=O1S[m], scalar=GCOL[:, 1:2], in1=RINV[m],
                        op0=ALU.mult, op1=ALU.mult),
                        reads=[("o1s", m), ("rinv", m), ("gcol",)], writes=[("act", 4 * c + t) for t in range(4)])
                defer(it + 1, lambda: m2(it + 1))
                defer(it + 2, m3)
                defer(it + 3, m4)

            for it in range(ns + 6):
                emit_conv()
                if it < ns:
                    s_z(it)
                    s_EL(it)
                if 1 <= it <= ns:
                    s_tri(it - 1)
                if 2 <= it <= ns + 1:
                    s_AV(it - 2, it)
                run_deferred(it)
            assert not deferred

            if stage < 4:
                raise _Stop()
            sc.guard([("wdown",)], QT_KEYS)
            sc.guard(UT_KEYS, V_KEYS)
            sc.guard(CBUF_KEYS, WORK_KEYS)
            sc.add("sp", lambda e: e.dma_start(out=gA[:, :], in_=g_mlp_d), writes=[("gA",)], dma="g")
            while conv_jobs:
                emit_conv()
            sc.add("pool", lambda e: e.dma_start(out=R3[:, 0:8192], in_=wout_sc[:, :]),
                   reads=WSC_KEYS, writes=[("wout",)], dma="wout")
            wup_n = [0]
            wdn_done = [False]

            def load_wup(fp):
                sl = wup_n[0] % 4
                wup_n[0] += 1
                sc.add("pool", lambda e, sl=sl, fp=fp: e.dma_start(out=R3[:, 8192 + sl * 2048: 8192 + (sl + 1) * 2048],
                                                                   in_=wup_sc[:, fp * 2048:(fp + 1) * 2048]),
                       reads=WSC_KEYS, writes=[("wup", sl)], dma="wup%d" % sl)
                if wup_n[0] == 4 and not wdn_done[0]:
                    wdn_done[0] = True
                    for f0 in range(4):
                        sc.add("pool", lambda e, f0=f0: e.dma_start(out=R1[:, f0 * 8192:(f0 + 1) * 8192],
                                                                   in_=wdn_sc[:, f0 * 8192:(f0 + 1) * 8192]),
                               reads=WSC_KEYS, writes=[("wdown",)], dma="wdown")
                return sl
            y_pairs = Rot([0, 1, 3])
            tpC = Rot([4, 5])
            up_banks = Rot([4, 5, 6, 7])
            dn_pairs = Rot([0, 1, 2, 3])
            rsl = Rot([0, 1])
            gscr = [abf[:, i, :].bitcast(F32) for i in range(2)]
            for c in range(NCH):
                pend_tr = None
                for t in range(5):
                    if t == 4:
                        transpose_to_actT(*pend_tr, rot=tpC)
                        break
                    T = 4 * c + t
                    slot = cnt["x"] % 2
                    cnt["x"] += 1
                    load_x(tok0 + T * 128, slot)
                    q = y_pairs.next()
                    for half in range(2):
                        bank = 2 * q + half

                        def f(e, bank=bank, T=T, half=half):
                            ins = None
                            for k in range(8):
                                ins = e.matmul(ps[:, bank, :], lhsT=ACTT[:, k, T * 128:(T + 1) * 128],
                                               rhs=WOUT[:, k, half * 512:(half + 1) * 512], start=(k == 0), stop=(k == 7))
                            return ins
                        sc.add("pe", f, reads=[("act", T), ("wout",)], writes=[("ps", bank)])
                        sc.add("dve", lambda e, bank=bank, t=t, half=half, slot=slot: e.tensor_tensor(
                            out=HB[:, t, half * 512:(half + 1) * 512], in0=ps[:, bank, :],
                            in1=xbuf[:, slot, half * 512:(half + 1) * 512], op=ALU.add),
                            reads=[("ps", bank), ("x", slot)], writes=[("h", t, half)])
                    rms_to_bf16(HB[:, t, :], [("h", t, 0), ("h", t, 1)], gA[:, :], ("gA",), abf[:, slot, :], ("abf", slot),
                                abf[:, slot, :], ("abf", slot))
                    if pend_tr is not None:
                        transpose_to_actT(*pend_tr, rot=tpC)
                    pend_tr = (abf[:, slot, :], ("abf", slot), T)
                mkeys = [("act", 4 * c + t) for t in range(4)]
                for fp in range(16):
                    sl = load_wup(fp)
                    for f4 in range(2):
                        fb = fp * 2 + f4
                        bank = up_banks.next()

                        def f(e, bank=bank, sl=sl, f4=f4, c=c):
                            ins = None
                            for k in range(8):
                                ins = e.matmul(ps[:, bank, :], lhsT=WUP4[sl][:, k, f4 * 128:(f4 + 1) * 128],
                                               rhs=ACTT[:, k, c * 512:(c + 1) * 512], start=(k == 0), stop=(k == 7))
                            return ins
                        sc.add("pe", f, reads=[("wup", sl)] + mkeys, writes=[("ps", bank)])
                        rsi = rsl.next()
                        rbuf = gscr[rsi]
                        sc.add("act", lambda e, bank=bank, rbuf=rbuf: e.activation(out=rbuf, in_=ps[:, bank, :], func=AF.Relu),
                               reads=[("ps", bank)], writes=[("abf", rsi)])
                        sc.add("dve", lambda e, rbuf=rbuf, fb=fb: e.tensor_tensor(out=UT[:, fb, :], in0=rbuf, in1=rbuf, op=ALU.mult),
                               reads=[("abf", rsi)], writes=[("uT", fb)])
                ukeys = [("uT", fb) for fb in range(32)]
                for t in range(4):
                    T = 4 * c + t
                    q = dn_pairs.next()
                    for half in range(2):
                        bank = 2 * q + half

                        def f(e, bank=bank, t=t, half=half):
                            ins = None
                            for fb in range(32):
                                ins = e.matmul(ps[:, bank, :], lhsT=UT[:, fb, t * 128:(t + 1) * 128],
                                               rhs=WDN[:, fb, half * 512:(half + 1) * 512], start=(fb == 0), stop=(fb == 31))
                            return ins
                        sc.add("pe", f, reads=ukeys + [("wdown",)], writes=[("ps", bank)])
                        sc.add("dve", lambda e, bank=bank, t=t, half=half: e.tensor_tensor(
                            out=HB[:, t, half * 512:(half + 1) * 512], in0=ps[:, bank, :],
                            in1=HB[:, t, half * 512:(half + 1) * 512], op=ALU.add),
                            reads=[("ps", bank), ("h", t, half)], writes=[("h", t, half)])
                    slot = cnt["x"] % 2
                    cnt["x"] += 1
                    rms_to_bf16(HB[:, t, :], [("h", t, 0), ("h", t, 1)], gF[:, :], ("gF",), xbuf[:, slot, :], ("x", slot),
                                abf[:, slot, :], ("abf", slot))
                    r0 = tok0 + T * 128
                    sc.add("sp", lambda e, slot=slot, r0=r0: e.dma_start(out=out[r0:r0 + 128, :], in_=xbuf[:, slot, :]),
                           reads=[("x", slot)], dma="o%d" % slot)

          except _Stop:
            break

        sc.emit(nc, block, sems, dma_sems, ["o0", "o1"])
    return nc


_NC_CACHE = {}


def make_in_maps(inputs, n_cores, nseq):
    cst, btab = make_consts()
    f = lambda a: np.ascontiguousarray(np.asarray(a, dtype=np.float32))
    x = f(inputs["x"])
    rep = lambda v: np.ascontiguousarray(np.broadcast_to(f(v).reshape(1, -1), (128, f(v).size)))
    lamv = rep(np.concatenate([f(inputs["lambda_q1"])[0], f(inputs["lambda_k1"])[0],
                               f(inputs["lambda_q2"])[0], f(inputs["lambda_k2"])[0]]))
    gcol = np.ascontiguousarray(np.stack([f(inputs["diff_subln"])[0], np.tile(f(inputs["sb_subln"])[0], 2)], axis=1))
    common = {
        "w_in": f(inputs["w_in"])[0], "w_out": f(inputs["w_out"])[0],
        "w_up": f(inputs["w_up"])[0], "w_down": f(inputs["w_down"])[0],
        "g_attn": rep(inputs["attn_norm"][0]), "g_mlp": rep(inputs["mlp_norm"][0]),
        "g_fin": rep(inputs["final_norm"]), "lamv": lamv, "gcol": gcol, "cst": cst, "btab": btab,
    }
    maps = []
    for c in range(n_cores):
        m = dict(common)
        m["x"] = np.ascontiguousarray(x[c * nseq:(c + 1) * nseq].reshape(nseq * S, D))
        maps.append(m)
    return maps


def kernel(**inputs):
    n_cores = 8
    nseq = 2
    if "nc" not in _NC_CACHE:
        _NC_CACHE["nc"] = build_nc(nseq)
    nc = _NC_CACHE["nc"]
    in_maps = make_in_maps(inputs, n_cores, nseq)
    res = run_bass_kernel_spmd(nc, in_maps, core_ids=list(range(n_cores)))
    outs = [np.asarray(r["out"]).reshape(nseq, S, D) for r in res.results]
    return np.concatenate(outs, axis=0).astype(np.float32)
```
